# Optimizing a Trainium2 kernel written in Bass

```python
import math
import jax, jax.numpy as jnp
from jax import lax
import numpy as np

D_MODEL = 1024
BATCH = 16
SEQ = 2048
DEPTH = 1

MLA_HEADS = 8
MLA_Q_RANK = 384
MLA_KV_RANK = 256
MLA_NOPE = 64
MLA_ROPE = 32
MLA_V = 64
ROPE_BASE = 10000.0
DIFF_HEADS = 4
DIFF_D = 64
IN_SPLITS = (MLA_Q_RANK, MLA_KV_RANK, MLA_ROPE,
             2 * DIFF_HEADS * DIFF_D, 2 * DIFF_HEADS * DIFF_D, DIFF_HEADS * 2 * DIFF_D)
IN_WIDTH = sum(IN_SPLITS)
IN_OFFSETS = tuple(int(v) for v in np.cumsum(IN_SPLITS)[:-1])
REL_BUCKETS = 32
REL_MAX_DIST = 128
MEM_LEN = 256
MEM_HEADS = 4
MEM_HEAD_DIM = D_MODEL // MEM_HEADS
D_FF = 2816
Q_BLOCK = 128
ALPHA = (2.0 * DEPTH) ** 0.25
BETA = (8.0 * DEPTH) ** -0.25
LN_EPS = 1e-5
RMS_EPS = 1e-6

kernel_name = "hybrid_mla_diffattn_macaron_deepnorm"


def layer_norm(x, g, b):
    xf = x.astype(jnp.float32)
    mu = jnp.mean(xf, axis=-1, keepdims=True)
    var = jnp.mean(jnp.square(xf - mu), axis=-1, keepdims=True)
    y = (xf - mu) * lax.rsqrt(var + LN_EPS) * g.astype(jnp.float32) + b.astype(jnp.float32)
    return y.astype(x.dtype)


def rms_norm(x, g):
    xf = x.astype(jnp.float32)
    y = xf * lax.rsqrt(jnp.mean(jnp.square(xf), axis=-1, keepdims=True) + RMS_EPS)
    return (y * g.astype(jnp.float32)).astype(x.dtype)


def swiglu(x, w1, w3, w2):
    return (jax.nn.silu(x @ w1) * (x @ w3)) @ w2


def rope(x, cos, sin):
    half = x.shape[-1] // 2
    x1, x2 = x[..., :half], x[..., half:]
    c, s = cos[None, :, None, :], sin[None, :, None, :]
    return jnp.concatenate([x1 * c - x2 * s, x2 * c + x1 * s], axis=-1)


def t5_bucket(q_pos, k_pos):
    n = jnp.maximum(q_pos[:, None] - k_pos[None, :], 0)
    max_exact = REL_BUCKETS // 2
    nf = jnp.maximum(n, 1).astype(jnp.float32)
    large = max_exact + (jnp.log(nf / max_exact) / math.log(REL_MAX_DIST / max_exact)
                         * (REL_BUCKETS - max_exact)).astype(jnp.int32)
    large = jnp.minimum(large, REL_BUCKETS - 1)
    return jnp.where(n < max_exact, n, large)


def parallel_mixer(h, rel_bias, w_in, q_norm_g, kv_norm_g, w_q_b, w_kv_b,
                   lam_q1, lam_k1, lam_q2, lam_k2, lam_init, diff_norm_g,
                   w_gate, b_gate, w_up_mla, w_up_diff, w_o):
    B, S, _ = h.shape
    proj = h @ w_in
    c_q, c_kv, k_rope, q_d, k_d, v_d = jnp.split(proj, IN_OFFSETS, axis=-1)

    q = (rms_norm(c_q, q_norm_g) @ w_q_b).reshape(B, S, MLA_HEADS, MLA_NOPE + MLA_ROPE)
    q_nope, q_rope = q[..., :MLA_NOPE], q[..., MLA_NOPE:]
    kv = (rms_norm(c_kv, kv_norm_g) @ w_kv_b).reshape(B, S, MLA_HEADS, MLA_NOPE + MLA_V)
    k_nope, v_m = kv[..., :MLA_NOPE], kv[..., MLA_NOPE:]
    pos = jnp.arange(S)
    inv_freq = ROPE_BASE ** (-jnp.arange(0, MLA_ROPE, 2, dtype=jnp.float32) / MLA_ROPE)
    ang = pos.astype(jnp.float32)[:, None] * inv_freq[None, :]
    cos, sin = jnp.cos(ang).astype(h.dtype), jnp.sin(ang).astype(h.dtype)
    q_rope = rope(q_rope, cos, sin)
    k_rope = rope(k_rope[:, :, None, :], cos, sin)[:, :, 0]

    q_d = q_d.reshape(B, S, 2 * DIFF_HEADS, DIFF_D)
    k_d = k_d.reshape(B, S, 2 * DIFF_HEADS, DIFF_D)
    v_d = v_d.reshape(B, S, DIFF_HEADS, 2 * DIFF_D)
    f32 = jnp.float32
    lam = (jnp.exp(jnp.sum(lam_q1.astype(f32) * lam_k1.astype(f32)))
           - jnp.exp(jnp.sum(lam_q2.astype(f32) * lam_k2.astype(f32))) + lam_init)

    mla_scale = (MLA_NOPE + MLA_ROPE) ** -0.5
    diff_scale = DIFF_D ** -0.5
    outs_m, outs_d = [], []
    for i in range(S // Q_BLOCK):
        lo, hi = i * Q_BLOCK, (i + 1) * Q_BLOCK
        q_pos = jnp.arange(lo, hi)
        k_pos = jnp.arange(hi)
        causal = q_pos[:, None] >= k_pos[None, :]
        s = (jnp.einsum('bqhd,bkhd->bhqk', q_nope[:, lo:hi], k_nope[:, :hi])
             + jnp.einsum('bqhr,bkr->bhqk', q_rope[:, lo:hi], k_rope[:, :hi]))
        p = jax.nn.softmax(jnp.where(causal, s.astype(f32) * mla_scale, -jnp.inf), axis=-1)
        outs_m.append(jnp.einsum('bhqk,bkhd->bqhd', p.astype(v_m.dtype), v_m[:, :hi]))
        bias = jnp.transpose(rel_bias[t5_bucket(q_pos, k_pos)], (2, 0, 1)).astype(f32)
        s = jnp.einsum('bqhd,bkhd->bhqk', q_d[:, lo:hi], k_d[:, :hi]).astype(f32) * diff_scale + bias
        p = jax.nn.softmax(jnp.where(causal, s, -jnp.inf), axis=-1)
        p = p.reshape(B, DIFF_HEADS, 2, Q_BLOCK, hi)
        a = p[:, :, 0] - lam * p[:, :, 1]
        outs_d.append(jnp.einsum('bhqk,bkhd->bqhd', a.astype(v_d.dtype), v_d[:, :hi]))

    o_m = jnp.concatenate(outs_m, axis=1).reshape(B, S, MLA_HEADS * MLA_V)
    o_d = jnp.concatenate(outs_d, axis=1)
    o_d = (rms_norm(o_d, diff_norm_g) * (1.0 - lam_init)).reshape(B, S, DIFF_HEADS * 2 * DIFF_D)

    y_m = o_m @ w_up_mla
    y_d = o_d @ w_up_diff
    g = jax.nn.sigmoid(h @ w_gate + b_gate)
    g_m, g_d = g[..., :D_MODEL], g[..., D_MODEL:]
    return (g_m * y_m + g_d * y_d) @ w_o


def memory_attention(h, mem, w_q, w_kv, w_o):
    B, S, _ = h.shape
    M = mem.shape[1]
    q = (h @ w_q).reshape(B, S, MEM_HEADS, MEM_HEAD_DIM)
    kv = mem @ w_kv
    k = kv[..., :D_MODEL].reshape(B, M, MEM_HEADS, MEM_HEAD_DIM)
    v = kv[..., D_MODEL:].reshape(B, M, MEM_HEADS, MEM_HEAD_DIM)
    s = jnp.einsum('bqhd,bkhd->bhqk', q, k).astype(jnp.float32) * MEM_HEAD_DIM ** -0.5
    p = jax.nn.softmax(s, axis=-1)
    o = jnp.einsum('bhqk,bkhd->bqhd', p.astype(v.dtype), v).reshape(B, S, D_MODEL)
    return o @ w_o


def setup_inputs(seed: int = 0) -> dict:
    key = jax.random.key(seed)
    ks = iter(jax.random.split(key, 64))
    L, D, F = DEPTH, D_MODEL, D_FF

    def dense(shape, fan_in, scale=1.0):
        return jax.random.normal(next(ks), shape, jnp.float32) * (fan_in ** -0.5) * scale

    def gain(shape):
        return 1.0 + 0.05 * jax.random.normal(next(ks), shape, jnp.float32)

    def small(shape, scale=0.05):
        return scale * jax.random.normal(next(ks), shape, jnp.float32)

    mla_in = MLA_HEADS * MLA_V
    diff_in = DIFF_HEADS * 2 * DIFF_D
    return {
        "x": jax.random.normal(next(ks), (BATCH, SEQ, D), jnp.float32),
        "mem": jax.random.normal(next(ks), (BATCH, MEM_LEN, D), jnp.float32),
        "rel_bias": small((REL_BUCKETS, 2 * DIFF_HEADS), 0.3),
        "ffn1_w1": dense((L, D, F), D),
        "ffn1_w3": dense((L, D, F), D),
        "ffn1_w2": dense((L, F, D), F, BETA),
        "ln1_g": gain((L, D)),
        "ln1_b": small((L, D)),
        "w_in": dense((L, D, IN_WIDTH), D),
        "q_norm_g": gain((L, MLA_Q_RANK)),
        "kv_norm_g": gain((L, MLA_KV_RANK)),
        "w_q_b": dense((L, MLA_Q_RANK, MLA_HEADS * (MLA_NOPE + MLA_ROPE)), MLA_Q_RANK),
        "w_kv_b": dense((L, MLA_KV_RANK, MLA_HEADS * (MLA_NOPE + MLA_V)), MLA_KV_RANK),
        "lam_q1": small((L, DIFF_D), 0.1),
        "lam_k1": small((L, DIFF_D), 0.1),
        "lam_q2": small((L, DIFF_D), 0.1),
        "lam_k2": small((L, DIFF_D), 0.1),
        "diff_norm_g": gain((L, 2 * DIFF_D)),
        "w_gate": dense((L, D, 2 * D), D),
        "b_gate": small((L, 2 * D), 0.1),
        "w_up_mla": dense((L, mla_in, D), mla_in, BETA),
        "w_up_diff": dense((L, diff_in, D), diff_in, BETA),
        "w_o": dense((L, D, D), D, BETA),
        "ln2_g": gain((L, D)),
        "ln2_b": small((L, D)),
        "mem_w_q": dense((L, D, D), D),
        "mem_w_kv": dense((L, D, 2 * D), D),
        "mem_w_o": dense((L, D, D), D, BETA),
        "ln3_g": gain((L, D)),
        "ln3_b": small((L, D)),
        "ffn2_w1": dense((L, D, F), D),
        "ffn2_w3": dense((L, D, F), D),
        "ffn2_w2": dense((L, F, D), F, BETA),
        "ln4_g": gain((L, D)),
        "ln4_b": small((L, D)),
    }


def reference(x, mem, rel_bias, ffn1_w1, ffn1_w3, ffn1_w2, ln1_g, ln1_b,
              w_in, q_norm_g, kv_norm_g, w_q_b, w_kv_b, lam_q1, lam_k1, lam_q2, lam_k2,
              diff_norm_g, w_gate, b_gate, w_up_mla, w_up_diff, w_o, ln2_g, ln2_b,
              mem_w_q, mem_w_kv, mem_w_o, ln3_g, ln3_b,
              ffn2_w1, ffn2_w3, ffn2_w2, ln4_g, ln4_b):
    h = x
    for l in range(DEPTH):
        lam_init = 0.8 - 0.6 * math.exp(-0.3 * l)
        h = layer_norm(ALPHA * h + 0.5 * swiglu(h, ffn1_w1[l], ffn1_w3[l], ffn1_w2[l]),
                       ln1_g[l], ln1_b[l])
        mix = parallel_mixer(h, rel_bias, w_in[l], q_norm_g[l], kv_norm_g[l], w_q_b[l], w_kv_b[l],
                             lam_q1[l], lam_k1[l], lam_q2[l], lam_k2[l], lam_init, diff_norm_g[l],
                             w_gate[l], b_gate[l], w_up_mla[l], w_up_diff[l], w_o[l])
        h = layer_norm(ALPHA * h + mix, ln2_g[l], ln2_b[l])
        h = layer_norm(ALPHA * h + memory_attention(h, mem, mem_w_q[l], mem_w_kv[l], mem_w_o[l]),
                       ln3_g[l], ln3_b[l])
        h = layer_norm(ALPHA * h + 0.5 * swiglu(h, ffn2_w1[l], ffn2_w3[l], ffn2_w2[l]),
                       ln4_g[l], ln4_b[l])
    return h
```

```python
import contextlib
import math
import numpy as np
import concourse.bass as bass
import concourse.mybir as mybir
from concourse.bass_utils import run_bass_kernel_spmd

F32 = mybir.dt.float32
BF16 = mybir.dt.bfloat16
AF = mybir.ActivationFunctionType
ALU = mybir.AluOpType

D = 1024
FF = 2816
SEQ = 2048
T = 512
NSUB = 4
MEM = 256
ALPHA = 2.0 ** 0.25
LN_EPS_EFF = 1e-5 / (ALPHA * ALPHA)
RMS_EPS = 1e-6
C_FFN = 0.5 / ALPHA
C_MIX = 1.0 / ALPHA
MLA_SCALE = 96.0 ** -0.5
MEM_SCALE = 256.0 ** -0.5
LAM_INIT = 0.8 - 0.6 * math.exp(-0.3 * 0)
NEG = -30000.0
NRING = 4
UNIT = 4096


class Sched:
    ENGS = ("pe", "act", "dve", "pool", "sp")

    def __init__(self, nc, same_engine_sync=True):
        self.nc = nc
        self.ops = {e: [] for e in self.ENGS}
        self.last_w = {}
        self.rd_e = {}
        self.rd_d = {}
        self.waited = {e: {} for e in self.ENGS}
        self.dma_sems = {}
        self.same_engine_sync = same_engine_sync
        self.marked = {e: set() for e in self.ENGS}

    def _need(self, eng, tok, waits):
        if tok is None:
            return
        if tok[0] == "e":
            _, pe, idx = tok
            if pe == eng and (eng in ("pe", "sp") or not self.same_engine_sync):
                return
            key = ("e", pe)
            if self.waited[eng].get(key, -1) >= idx:
                return
            self.waited[eng][key] = idx
            waits.append(tok)
            self.marked[pe].add(idx)
        else:
            _, sname, val = tok
            key = ("d", sname)
            if self.waited[eng].get(key, -1) >= val:
                return
            self.waited[eng][key] = val
            waits.append(tok)

    def _deps(self, eng, reads, writes):
        waits = []
        for r in reads:
            self._need(eng, self.last_w.get(r), waits)
        for w in writes:
            self._need(eng, self.last_w.get(w), waits)
            for pe, idx in self.rd_e.get(w, {}).items():
                self._need(eng, ("e", pe, idx), waits)
            for t in self.rd_d.get(w, ()):
                self._need(eng, t, waits)
        return waits

    def _commit(self, tok, reads, writes):
        for r in reads:
            if tok[0] == "e":
                d = self.rd_e.setdefault(r, {})
                d[tok[1]] = max(d.get(tok[1], -1), tok[2])
            else:
                self.rd_d.setdefault(r, []).append(tok)
        for w in writes:
            self.last_w[w] = tok
            self.rd_e[w] = {}
            self.rd_d[w] = []

    def op(self, eng, fn, reads=(), writes=()):
        waits = self._deps(eng, reads, writes)
        idx = len(self.ops[eng])
        self.ops[eng].append(("op", fn, waits, None))
        self._commit(("e", eng, idx), reads, writes)

    def dma(self, eng, fn, sem, reads=(), writes=()):
        waits = self._deps(eng, reads, writes)
        c = self.dma_sems.setdefault(sem, [0])
        c[0] += 16
        tok = ("d", sem, c[0])
        self.ops[eng].append(("dma", fn, waits, sem))
        self._commit(tok, reads, writes)
        return tok

    def alias(self, new_names, old_names):
        re, rdm = {}, []
        for o in old_names:
            t = self.last_w.get(o)
            if t is not None:
                if t[0] == "e":
                    re[t[1]] = max(re.get(t[1], -1), t[2])
                else:
                    rdm.append(t)
            for pe, idx in self.rd_e.get(o, {}).items():
                re[pe] = max(re.get(pe, -1), idx)
            rdm.extend(self.rd_d.get(o, ()))
        for n in new_names:
            self.last_w[n] = None
            d = self.rd_e.setdefault(n, {})
            for pe, idx in re.items():
                d[pe] = max(d.get(pe, -1), idx)
            self.rd_d.setdefault(n, []).extend(rdm)

    def wait_all(self, eng, toks):
        waits = []
        for t in toks:
            self._need(eng, t, waits)
        self.ops[eng].append(("wait", None, waits, None))

    def emit(self):
        nc = self.nc
        with contextlib.ExitStack() as st:
            esem = {e: st.enter_context(nc.semaphore("s_" + e)) for e in self.ENGS}
            dsem = {n: st.enter_context(nc.semaphore("d_" + str(n))) for n in self.dma_sems}
            block = st.enter_context(nc.Block())
            cum = {}
            for e in self.ENGS:
                c = 0
                arr = []
                for i in range(len(self.ops[e])):
                    if i in self.marked[e]:
                        c += 1
                    arr.append(c)
                cum[e] = arr

            def run(e, engobj):
                for i, (kind, fn, waits, sem) in enumerate(self.ops[e]):
                    for t in waits:
                        if t[0] == "e":
                            engobj.wait_ge(esem[t[1]], cum[t[1]][t[2]])
                        else:
                            engobj.wait_ge(dsem[t[1]], t[2])
                    if kind == "wait":
                        continue
                    ins = fn(engobj)
                    if kind == "dma":
                        ins.then_inc(dsem[sem], 16)
                    elif i in self.marked[e]:
                        ins.then_inc(esem[e], 1)

            @block.tensor
            def _(eng):
                run("pe", eng)

            @block.scalar
            def _(eng):
                run("act", eng)

            @block.vector
            def _(eng):
                run("dve", eng)

            @block.gpsimd
            def _(eng):
                run("pool", eng)

            @block.sync
            def _(eng):
                run("sp", eng)


W_SHAPES = {
    "ffn1_w1": (D, FF), "ffn1_w3": (D, FF), "ffn1_w2": (FF, D),
    "w_in": (D, 2208), "w_q_b": (384, 768), "w_kv_b": (256, 1024),
    "w_gate": (D, 2 * D), "w_up_mla": (512, D), "w_up_diff": (512, D), "w_o": (D, D),
    "mem_w_q": (D, D), "mem_w_kv": (D, 2 * D), "mem_w_o": (D, D),
    "ffn2_w1": (D, FF), "ffn2_w3": (D, FF), "ffn2_w2": (FF, D),
}


def unit_table():
    units = {}
    order = []

    def add(name, nk, ncols, group, pieces):
        assert nk * ncols <= UNIT, (name, nk, ncols)
        units[name] = (len(order), nk, ncols, group, pieces)
        order.append(name)

    for pre in ("ffn1", "ffn2"):
        for g in range(6):
            c0 = g * 512
            c1 = min(FF, c0 + 512)
            add(f"{pre}_w1_{g}", 8, c1 - c0, pre + "_w1", [(pre + "_w1", 0, 8, (c0, c1), 0)])
            add(f"{pre}_w3_{g}", 8, c1 - c0, pre + "_w3", [(pre + "_w3", 0, 8, (c0, c1), 0)])
        for g in range(6):
            f0 = g * 4
            f1 = min(22, f0 + 4)
            add(f"{pre}_w2_{g}", f1 - f0, 1024, pre + "_w2", [(pre + "_w2", f0, f1, (0, 1024), 0)])
    add("win_cq", 8, 384, "w_in", [("w_in", 0, 8, (0, 384), 0)])
    add("win_ckv", 8, 256, "w_in", [("w_in", 0, 8, (384, 640), 0)])
    add("win_kr", 8, 192, "w_in", [("w_in", 0, 8, (576, 672), 0),
                                    ("w_in", 0, 8, (576, 640), 96),
                                    ("w_in", 0, 8, (656, 672), 160),
                                    ("w_in", 0, 8, (640, 656), 176)])
    add("win_qd", 8, 512, "w_in", [("w_in", 0, 8, (672, 1184), 0)])
    add("win_kd", 8, 512, "w_in", [("w_in", 0, 8, (1184, 1696), 0)])
    add("win_vd", 8, 512, "w_in", [("w_in", 0, 8, (1696, 2208), 0)])
    add("wqb", 3, 768, "w_q_b", [("w_q_b", 0, 3, (0, 768), 0)])
    add("wqbs", 3, 768, "w_q_b", [("ROPESWAP",)])
    add("wkvb_k", 2, 512, "w_kv_b", [("HEADGATHER", "w_kv_b", 0)])
    add("wkvb_v", 2, 512, "w_kv_b", [("HEADGATHER", "w_kv_b", 64)])
    for g in range(4):
        add(f"wgate_{g}", 8, 512, "w_gate", [("w_gate", 0, 8, (g * 512, g * 512 + 512), 0)])
    for g in range(2):
        add(f"wup_{g}", 8, 512, "w_up", [("w_up_mla", 0, 4, (g * 512, g * 512 + 512), 0),
                                          ("w_up_diff", 0, 4, (g * 512, g * 512 + 512), 4 * 512)])
    for g in range(2):
        add(f"wo_{g}", 4, 1024, "w_o", [("w_o", g * 4, g * 4 + 4, (0, 1024), 0)])
    for g in range(2):
        add(f"mq_{g}", 8, 512, "mem_w_q", [("mem_w_q", 0, 8, (g * 512, g * 512 + 512), 0)])
    for g in range(4):
        add(f"mkv_{g}", 8, 512, "mem_w_kv", [("mem_w_kv", 0, 8, (g * 512, g * 512 + 512), 0)])
    for g in range(2):
        add(f"mo_{g}", 4, 1024, "mem_w_o", [("mem_w_o", g * 4, g * 4 + 4, (0, 1024), 0)])
    return units, order


def build(n_seq=2, n_tiles=4, stop_stage=4):
    nc = bass.Bass("TRN2", target_bir_lowering=False)
    ntok = n_seq * SEQ
    x_d = nc.dram_tensor("x", [ntok, D], F32, kind="ExternalInput").ap()
    mem_d = nc.dram_tensor("mem", [n_seq * MEM, D], F32, kind="ExternalInput").ap()
    out_d = nc.dram_tensor("out", [ntok, D], F32, kind="ExternalOutput").ap()
    wd = {n: nc.dram_tensor(n, list(s), F32, kind="ExternalInput").ap() for n, s in W_SHAPES.items()}
    lng_d = nc.dram_tensor("lnG", [4, 128, D], F32, kind="ExternalInput").ap()
    lnb_d = nc.dram_tensor("lnB", [4, 128, D], F32, kind="ExternalInput").ap()
    cols_d = nc.dram_tensor("cols", [128, 96], F32, kind="ExternalInput").ap()
    lamv_d = nc.dram_tensor("lamv", [128, 4, 64], F32, kind="ExternalInput").ap()
    cs_d = nc.dram_tensor("cs", [128, 2, SEQ], F32, kind="ExternalInput").ap()
    tz_d = nc.dram_tensor("tz", [128, 8, 256], F32, kind="ExternalInput").ap()
    msk_d = nc.dram_tensor("msk", [128, 128], F32, kind="ExternalInput").ap()
    idn_d = nc.dram_tensor("idn", [128, 128], F32, kind="ExternalInput").ap()
    units, uorder = unit_table()
    NU = len(uorder)
    scr = nc.dram_tensor("scr", [NU, 128, UNIT], BF16, kind="Internal").ap()

    S = Sched(nc)
    st = contextlib.ExitStack()
    with st:
        def sb(name, shape, dt):
            return st.enter_context(nc.sbuf_tensor("sb_" + name, shape, dt))

        h = sb("h", [128, NSUB, D], F32)
        hT = sb("hT", [128, 8, T], BF16)
        KT = sb("KT", [128, 8, SEQ], BF16)
        Vm = sb("Vm", [128, 16, 512], BF16)
        KdT = sb("KdT", [128, 4, SEQ], BF16)
        Vd = sb("Vd", [128, 16, 512], BF16)
        KmT = sb("KmT", [128, 8, MEM], BF16)
        Vmem = sb("Vmem", [128, 2, D], BF16)
        arenaA = sb("arenaA", [128, 12288], BF16)
        arenaW = sb("arenaW", [128, 2048], F32)
        ring = sb("ring", [128, NRING, UNIT], BF16)
        lnGB = sb("lnGB", [128, 2, D], F32)
        cs = sb("cs", [128, 2, T], F32)
        tz = sb("tz", [128, 8, 256], F32)
        colsb = sb("colsb", [128, 96], F32)
        lamv = sb("lamv", [128, 4, 64], F32)
        lamt = sb("lamt", [128, 64], F32)
        small = sb("small", [128, 32], F32)
        identf = sb("identf", [128, 128], F32)
        identb = sb("identb", [128, 128], BF16)
        onesb = sb("onesb", [128, 128], BF16)
        mskf = sb("mskf", [128, 128], F32)
        mskb = sb("mskb", [128, 128], BF16)
        epsc = sb("epsc", [128, 8], F32)
        st6 = sb("st6", [128, 2, 2, 6], F32)
        mv = sb("mv", [128, 2, 8], F32)

        PS = [st.enter_context(nc.psum_tensor(f"ps{i}", [128, 512], F32)) for i in range(8)]

        def PB(i):
            return ("ps", i)

        gT = arenaA[:, 0:22 * T].rearrange("p (f t) -> p f t", f=22)
        QT = arenaA[:, 0:4096].rearrange("p (a t) -> p a t", a=8)
        QdT = arenaA[:, 4096:6144].rearrange("p (a t) -> p a t", a=4)
        omT = arenaA[:, 8192:10240].rearrange("p (a t) -> p a t", a=4)
        odT = arenaA[:, 10240:12288].rearrange("p (a t) -> p a t", a=4)
        mixT = arenaA[:, 0:4096].rearrange("p (a t) -> p a t", a=8)
        qmT = arenaA[:, 4096:8192].rearrange("p (a t) -> p a t", a=8)
        omemT = arenaA[:, 8192:12288].rearrange("p (a t) -> p a t", a=8)
        cqg = arenaA[:, 6144:7680].rearrange("p (a t) -> p a t", a=3)
        RCv = arenaA[:, 8192:9216].bitcast(F32)
        RSv = arenaA[:, 9216:10240].bitcast(F32)
        tav = arenaA[:, 10240:11264].bitcast(F32)
        tbv = arenaA[:, 11264:12288].bitcast(F32)
        sbuf2 = [arenaW[:, 0:512], arenaW[:, 512:1024]]
        ckvg = arenaW[:, 1024:1536].bitcast(BF16).rearrange("p (a t) -> p a t", a=2)
        sqb = arenaW[:, 1536:2048].bitcast(BF16).rearrange("p (a t) -> p a t", a=2)
        Rq = arenaW[:, 0:512]
        Rkv = arenaW[:, 512:1024]
        PT = [arenaW[:, 0:256].bitcast(BF16), arenaW[:, 256:512].bitcast(BF16), arenaW[:, 512:768].bitcast(BF16)]
        recb = arenaW[:, 768:1280]
        on = [arenaW[:, 1280:1792]]
        sqd = arenaW[:, 1792:2048].bitcast(BF16)
        gbuf = [arenaW[:, i * 512:(i + 1) * 512] for i in range(4)]

        def mm(out, lhsT, rhs, start, stop, reads, writes, skip=False):
            if skip:
                S.op("pe", lambda e: e.matmul(out, lhsT=lhsT, rhs=rhs, start=start, stop=stop, skip_group_check=True),
                     reads=reads, writes=writes)
            else:
                S.op("pe", lambda e: e.matmul(out, lhsT=lhsT, rhs=rhs, start=start, stop=stop),
                     reads=reads, writes=writes)

        def act(out, in_, func, reads, writes, bias=None, scale=None):
            kw = {}
            if bias is not None:
                kw["bias"] = bias
            if scale is not None:
                kw["scale"] = scale
            S.op("act", lambda e: e.activation(out=out, in_=in_, func=func, **kw), reads=reads, writes=writes)

        def tt(eng, out, in0, in1, op, reads, writes):
            S.op(eng, lambda e: e.tensor_tensor(out=out, in0=in0, in1=in1, op=op), reads=reads, writes=writes)

        def stt(eng, out, in0, scalar, in1, op0, op1, reads, writes):
            S.op(eng, lambda e: e.scalar_tensor_tensor(out=out, in0=in0, scalar=scalar, in1=in1, op0=op0, op1=op1),
                 reads=reads, writes=writes)

        def ts(eng, out, in0, s1, s2, op0, op1, reads, writes):
            if s2 is None:
                S.op(eng, lambda e: e.tensor_scalar(out=out, in0=in0, scalar1=s1, scalar2=None, op0=op0),
                     reads=reads, writes=writes)
            else:
                S.op(eng, lambda e: e.tensor_scalar(out=out, in0=in0, scalar1=s1, scalar2=s2, op0=op0, op1=op1),
                     reads=reads, writes=writes)

        def rsqrt(out, in_, epscol, reads, writes):
            act(out, in_, AF.Sqrt, list(reads) + ["epsc"], list(writes), bias=epscol)
            S.op("dve", lambda e: e.reciprocal(out=out, in_=out), reads=list(writes), writes=list(writes))

        def recip_act(out, in_, reads, writes):
            act(out, in_, AF.Ln, list(reads), list(writes))
            act(out, out, AF.Exp, list(writes), list(writes), scale=-1.0)

        def rsqrt_act(out, in_, epscol, reads, writes, lnscale=None):
            act(out, in_, AF.Ln, list(reads) + ["epsc"], list(writes), bias=epscol)
            if lnscale is None:
                act(out, out, AF.Exp, list(writes), list(writes), scale=-0.5)
            else:
                act(out, out, AF.Exp, list(writes) + ["epsc"], list(writes), scale=-0.5, bias=lnscale)

        def cp(eng, out, in_, reads, writes):
            if eng == "act":
                S.op("act", lambda e: e.copy(out=out, in_=in_), reads=reads, writes=writes)
            else:
                S.op(eng, lambda e: e.tensor_copy(out=out, in_=in_), reads=reads, writes=writes)

        def dma(out, in_, sem, reads, writes, eng="sp"):
            return S.dma(eng, lambda e: e.dma_start(out=out, in_=in_), sem, reads=reads, writes=writes)

        group_units = {}
        for name in uorder:
            group_units.setdefault(units[name][3], []).append(name)

        def emit_casts(groups, after=()):
            after = list(after)
            for group in groups:
                for name in group_units[group]:
                    idx, nk, ncols, _, pieces = units[name]
                    dst3 = scr[idx, :, 0:nk * ncols].rearrange("p (k c) -> p k c", k=nk)
                    sem = f"cu{idx}"
                    wr = [("scr", name)]
                    for pc in pieces:
                        if pc[0] == "ROPESWAP":
                            w = wd["w_q_b"].rearrange("(k p) (hh c) -> p k hh c", p=128, c=96)
                            d4 = scr[idx, :, 0:nk * ncols].rearrange("p (k hh c) -> p k hh c", k=3, c=96)
                            for kk in range(3):
                                dma(d4[:, kk, :, 0:64], w[:, kk, :, 0:64], sem, after, wr, eng="pool")
                                dma(d4[:, kk, :, 64:80], w[:, kk, :, 80:96], sem, after, wr, eng="pool")
                                dma(d4[:, kk, :, 80:96], w[:, kk, :, 64:80], sem, after, wr, eng="pool")
                        elif pc[0] == "HEADGATHER":
                            w = wd[pc[1]].rearrange("(k p) (hh c) -> p k hh c", p=128, c=128)
                            d4 = scr[idx, :, 0:nk * ncols].rearrange("p (k hh c) -> p k hh c", k=2, c=64)
                            for kk in range(2):
                                dma(d4[:, kk, :, :], w[:, kk, :, pc[2]:pc[2] + 64], sem, after, wr, eng="pool")
                        else:
                            wname, k0, k1, (c0, c1), dc0 = pc
                            src = wd[wname][k0 * 128:k1 * 128, c0:c1].rearrange("(k p) c -> p k c", p=128)
                            if len(pieces) == 1:
                                dst = dst3
                            elif name.startswith("wup_"):
                                kb = dc0 // 512
                                dst = dst3[:, kb:kb + (k1 - k0), :]
                            else:
                                dst = dst3[:, :, dc0:dc0 + (c1 - c0)]
                            dma(dst, src, sem, after, wr, eng="pool")
                for n in group_units[group]:
                    sem = f"cu{units[n][0]}"
                    S.last_w[("scr", n)] = ("d", sem, S.dma_sems[sem][0])

        CAST_ORDER = ["ffn1_w1", "ffn1_w3", "ffn1_w2", "w_in", "w_q_b", "w_kv_b", "w_gate", "w_up", "w_o",
                      "mem_w_kv", "mem_w_q", "mem_w_o", "ffn2_w1", "ffn2_w3", "ffn2_w2"]

        ring_ctr = [0]

        def load_unit(name):
            idx, nk, ncols, group, _ = units[name]
            slot = ring_ctr[0] % NRING
            ring_ctr[0] += 1
            n = nk * ncols
            dma(ring[:, slot, 0:n], scr[idx, :, 0:n], f"ring{slot}", [("scr", name)], [("ring", slot)])
            return ring[:, slot, 0:n].rearrange("p (k c) -> p k c", k=nk), ("ring", slot)

        dma(colsb[:], cols_d, "c_cols", [], ["colsb"])
        dma(lamv[:], lamv_d, "c_lamv", [], ["lamv"])
        dma(tz[:], tz_d, "c_tz", [], ["tz"])
        dma(mskf[:], msk_d, "c_msk", [], ["mskf"])
        dma(identf[:], idn_d, "c_idn", [], ["identf"])
        cp("dve", identb[:], identf[:], ["identf"], ["identb"])
        cp("dve", mskb[:], mskf[:], ["mskf"], ["mskb"])
        S.op("dve", lambda e: e.memset(onesb[:], 1.0), writes=["onesb"])
        for i_, v_ in enumerate((LN_EPS_EFF, 384.0 * RMS_EPS, 256.0 * RMS_EPS, 128.0 * RMS_EPS,
                                 0.5 * math.log(384.0), 0.5 * math.log(256.0))):
            S.op("dve", lambda e, i_=i_, v_=v_: e.memset(epsc[:, i_:i_ + 1], v_), writes=["epsc"])
        tt("dve", lamt[:], lamv[:, 0, :], lamv[:, 1, :], ALU.mult, ["lamv"], ["lamt"])
        S.op("dve", lambda e: e.reduce_sum(out=small[:, 0:1], in_=lamt[:], axis=mybir.AxisListType.X),
             reads=["lamt"], writes=["small0"])
        tt("dve", lamt[:], lamv[:, 2, :], lamv[:, 3, :], ALU.mult, ["lamv"], ["lamt"])
        S.op("dve", lambda e: e.reduce_sum(out=small[:, 1:2], in_=lamt[:], axis=mybir.AxisListType.X),
             reads=["lamt"], writes=["small1"])
        act(small[:, 2:4], small[:, 0:2], AF.Exp, ["small0", "small1"], ["small23"])
        tt("dve", small[:, 4:5], small[:, 2:3], small[:, 3:4], ALU.subtract, ["small23"], ["small4"])
        ts("dve", small[:, 4:5], small[:, 4:5], LAM_INIT, None, ALU.add, ALU.bypass, ["small4"], ["small4"])
        ts("dve", small[:, 5:6], small[:, 4:5], -1.0, None, ALU.mult, ALU.bypass, ["small4"], ["nlam"])
        ts("dve", small[:, 6:7], colsb[:, 21:22], (1.0 - LAM_INIT) * math.sqrt(128.0), None, ALU.mult, ALU.bypass,
           ["colsb"], ["gd2"])
        nlam = small[:, 5:6]
        gd2 = small[:, 6:7]
        QNG = lambda m: colsb[:, m:m + 1]
        KVNG = lambda m: colsb[:, 3 + m:4 + m]
        BGATE = lambda c: colsb[:, 5 + c:6 + c]
        B31 = lambda m: colsb[:, 22 + m:23 + m]

        def run_pipeline(items, L=1, Dd=2):
            deferred = []
            n = len(items)
            for i in range(n + L):
                if i < n:
                    items[i]["S"]()
                    items[i]["E"]()
                if i >= L:
                    it = items[i - L]
                    it["V"]()
                    if it.get("post"):
                        dfn = it["post"]()
                        if dfn is not None:
                            deferred.append((i + Dd, dfn))
                due = [d for d in deferred if d[0] <= i]
                deferred = [d for d in deferred if d[0] > i]
                for _, fn in due:
                    fn()
            for _, fn in deferred:
                fn()

        HT_ALL = [("hT", s) for s in range(NSUB)]
        tr_ctr = [0]

        def make_hT(sub, bank0, ln_idx=None):
            for k in range(8):
                b = bank0 + k // 4
                S.op("pe", lambda e, k=k, b=b: e.transpose(out=PS[b][:, (k % 4) * 128:(k % 4 + 1) * 128],
                                                           in_=h[:, sub, k * 128:(k + 1) * 128], identity=identf[:]),
                     reads=[("h", sub), "identf"], writes=[PB(b)])
            for k in range(8):
                b = bank0 + k // 4
                src = PS[b][:, (k % 4) * 128:(k % 4 + 1) * 128]
                dst = hT[:, k, sub * 128:(sub + 1) * 128]
                if ln_idx is None:
                    cp("act" if k % 2 else "dve", dst, src, [PB(b)], [("hT", sub)])
                else:
                    gc = colsb[:, 32 + ln_idx * 8 + k:33 + ln_idx * 8 + k]
                    bc = colsb[:, 64 + ln_idx * 8 + k:65 + ln_idx * 8 + k]
                    if k % 2:
                        act(dst, src, AF.Identity, [PB(b), "colsb"], [("hT", sub)], bias=bc, scale=gc)
                    else:
                        ts("dve", dst, src, gc, bc, ALU.mult, ALU.add, [PB(b), "colsb"], [("hT", sub)])

        def load_lngb(i):
            dma(lnGB[:, 0, :], lng_d[i], "lnG", [], ["lnG"], eng="pool")
            dma(lnGB[:, 1, :], lnb_d[i], "lnB", [], ["lnB"], eng="pool")

        class LNPipe:
            def __init__(self, c, final, tile_row0, ln_idx, after_D=None):
                self.c, self.final, self.row0, self.i, self.ln_idx = c, final, tile_row0, 0, ln_idx
                self.after_D = after_D
                self.after_A = None

            def A(self, sub):
                hs = ("h", sub)
                c = self.c
                for half in range(2):
                    b = 2 * sub + half
                    stt("dve", h[:, sub, half * 512:(half + 1) * 512], PS[b][:], c, h[:, sub, half * 512:(half + 1) * 512],
                        ALU.mult, ALU.add, [PB(b), hs], [hs])
                k = sub % 2
                for half in range(2):
                    S.op("dve", lambda e, half=half: e.bn_stats(out=st6[:, k, half, :], in_=h[:, sub, half * 512:(half + 1) * 512]),
                         reads=[hs], writes=[("st6", k)])
                S.op("dve", lambda e: e.bn_aggr(out=mv[:, k, 0:2], in_=st6[:, k, :, :]), reads=[("st6", k)], writes=[("mv", k)])

            def hookA(self, sub):
                if self.after_A is not None:
                    self.after_A(sub)

            def A2(self, sub):
                hs = ("h", sub)
                k = sub % 2
                mk = [("mv", k)]
                act(mv[:, k, 2:3], mv[:, k, 1:2], AF.Ln, mk + ["epsc"], [("mvr", k)], bias=epsc[:, 0:1])
                act(mv[:, k, 2:3], mv[:, k, 2:3], AF.Exp, [("mvr", k)], [("mvr", k)], scale=-0.5)
                S.op("act", lambda e: e.mul(out=mv[:, k, 3:4], in_=mv[:, k, 2:3], mul=-1.0), reads=[("mvr", k)], writes=[("mvn", k)])
                act(mv[:, k, 4:5], mv[:, k, 0:1], AF.Copy, mk + [("mvn", k)], [("mvb", k)], scale=mv[:, k, 3:4])
                act(h[:, sub, :], h[:, sub, :], AF.Identity, [hs, ("mvr", k), ("mvb", k)], [hs], bias=mv[:, k, 4:5], scale=mv[:, k, 2:3])

            def B(self, sub):
                hs = ("h", sub)
                tt("pool", h[:, sub, :], h[:, sub, :], lnGB[:, 0, :], ALU.mult, [hs, "lnG"], [hs])
                tt("pool", h[:, sub, :], h[:, sub, :], lnGB[:, 1, :], ALU.add, [hs, "lnB"], [hs])

            def C(self, sub):
                r0 = self.row0 + sub * 128
                dma(out_d[r0:r0 + 128, :], h[:, sub, :], f"out{sub}", [("h", sub)], [("outd", sub)], eng="pool")

            def D(self, sub):
                make_hT(sub, 2 * sub, self.ln_idx)
                if self.after_D is not None:
                    self.after_D(sub)

            def step(self):
                i = self.i
                self.i += 1
                if self.final:
                    order = [(0, self.A), (0, self.hookA), (0, self.A2), (1, self.B), (2, self.C)]
                else:
                    order = [(0, self.A), (2, self.D), (0, self.A2), (3, self.B)]
                for k, fn in order:
                    sub = i - k
                    if 0 <= sub < NSUB:
                        fn(sub)

            def flush(self):
                while self.i < NSUB + 3:
                    self.step()

        def down_proj(uname, actT, actname, ln_cb):
            U0, r0 = load_unit(f"{uname}_0")
            U1, r1 = load_unit(f"{uname}_1")
            for sub in range(NSUB):
                for k in range(8):
                    U_, r_ = (U0, r0) if k < 4 else (U1, r1)
                    for half in range(2):
                        b = 2 * sub + half
                        mm(PS[b][:], actT[:, k, sub * 128:(sub + 1) * 128], U_[:, k % 4, half * 512:(half + 1) * 512],
                           k == 0, k == 7, [(actname, k), r_], [PB(b)])
                ln_cb.step()
            ln_cb.flush()

        def cast_one(name, after=()):
            g = units[name][3]
            saved = group_units[g]
            group_units[g] = [name]
            emit_casts([g], after=after)
            group_units[g] = saved

        def emit_casts_interleaved(pres, after=()):
            for pre in pres:
                for g in range(6):
                    cast_one(f"{pre}_w1_{g}", after)
                    cast_one(f"{pre}_w3_{g}", after)
                for g in range(6):
                    cast_one(f"{pre}_w2_{g}", after)

        tile_rows = [seq_ * SEQ + t_ * T for seq_ in range(n_seq) for t_ in range(n_tiles)]

        xb = arenaW[:].bitcast(BF16).rearrange("p (a t) -> p a t", a=4)

        def load_xb(n):
            r0 = tile_rows[n]
            S.alias(["xb"], AR_W_NAMES)
            dma(xb, x_d[r0:r0 + T, :].rearrange("(s p) d -> p s d", p=128), "xb", [], ["xb"], eng="pool")

        def next_T(sub):
            bank = PS[2 * sub].bitcast(BF16)
            for k in range(8):
                S.op("pe", lambda e, k=k: e.transpose(out=bank[:, k * 128:(k + 1) * 128],
                                                      in_=xb[:, sub, k * 128:(k + 1) * 128], identity=identb[:]),
                     reads=["xb", "identb"], writes=[PB(2 * sub)])
            cp("act" if sub % 2 else "dve", hT[:, :, sub * 128:(sub + 1) * 128],
               bank.rearrange("p (k t) -> p k t", k=8), [PB(2 * sub)], [("hT", sub)])

        next_hook = [None]
        cast_hook = [None, None]

        def ffn(pre, ln_cb):
            S.alias([("gT", f) for f in range(22)], ["arenaA_all"] + AR_A_NAMES)
            S.alias(["silu0", "silu1"], AR_W_NAMES)
            pair = 0
            for g in range(6):
                ncol = 512 if g < 5 else 256
                U1, r1 = load_unit(f"{pre}_w1_{g}")
                U3, r3 = load_unit(f"{pre}_w3_{g}")
                for fl in range(ncol // 128):
                    f = g * 4 + fl
                    ba, bb = 2 * (pair % 4), 2 * (pair % 4) + 1
                    pair += 1
                    for k in range(8):
                        mm(PS[ba][:], U1[:, k, fl * 128:(fl + 1) * 128], hT[:, k, :], k == 0, k == 7,
                           [r1] + HT_ALL, [PB(ba)])
                    for k in range(8):
                        mm(PS[bb][:], U3[:, k, fl * 128:(fl + 1) * 128], hT[:, k, :], k == 0, k == 7,
                           [r3] + HT_ALL, [PB(bb)])
                    sname = f"silu{f % 2}"
                    act(sbuf2[f % 2], PS[ba][:], AF.Silu, [PB(ba)], [sname])
                    tt("dve", gT[:, f, :], sbuf2[f % 2], PS[bb][:], ALU.mult, [sname, PB(bb)], [("gT", f)])
                    if f == 0 and cast_hook[0] is not None:
                        cast_hook[0]()
                    if f == 21 and cast_hook[1] is not None:
                        cast_hook[1]()
            if next_hook[0] is not None:
                next_hook[0]()
                next_hook[0] = None
            for g in range(3):
                U2, r2 = load_unit(f"{pre}_w2_{g}")
                for fl in range(4):
                    f = g * 4 + fl
                    for sub in range(NSUB):
                        for half in range(2):
                            b = 2 * sub + half
                            mm(PS[b][:], gT[:, f, sub * 128:(sub + 1) * 128], U2[:, fl, half * 512:(half + 1) * 512],
                               f == 0, False, [("gT", f), r2], [PB(b)])
            U2a, r2a = load_unit(f"{pre}_w2_3")
            U2b, r2b = load_unit(f"{pre}_w2_4")
            U2c, r2c = load_unit(f"{pre}_w2_5")
            tail = ([(U2a, r2a, fl, 12 + fl) for fl in range(4)] + [(U2b, r2b, fl, 16 + fl) for fl in range(4)]
                    + [(U2c, r2c, fl, 20 + fl) for fl in range(2)])
            for sub in range(NSUB):
                for (U2, r2, fl, f) in tail:
                    for half in range(2):
                        b = 2 * sub + half
                        mm(PS[b][:], gT[:, f, sub * 128:(sub + 1) * 128], U2[:, fl, half * 512:(half + 1) * 512],
                           False, f == 21, [("gT", f), r2], [PB(b)])
                ln_cb.step()
            ln_cb.flush()

        AR_A_NAMES = ([("gT", f) for f in range(22)] + [("QT", a) for a in range(8)] + ["cqg"]
                      + [("QdT", a) for a in range(4)] + [("omT", a) for a in range(4)] + [("odT", a) for a in range(4)]
                      + [("mixT", a) for a in range(8)] + [("qmT", a) for a in range(8)] + [("omemT", a) for a in range(8)]
                      + ["memf", "memb", "memT", "RC", "RS", "ta", "tb"])
        AR_W_NAMES = ["xb", "silu0", "silu1", ("ckvg", 0), ("ckvg", 1), ("sqb", 0), ("sqb", 1), "Rq", "Rkv",
                      "PT0", "PT1", "PT2", "recb", "on0", "sqd",
                      "g0", "g1", "g2", "g3"]

        def mixer(t, col0, ln_cb):
            S.alias([("QT", a) for a in range(8)] + [("QdT", a) for a in range(4)]
                    + ["RC", "RS", "ta", "tb", "cqg"], AR_A_NAMES)
            S.alias([("ckvg", 0), ("ckvg", 1), ("sqb", 0), ("sqb", 1), "Rq", "Rkv"], AR_W_NAMES)
            dma(cs[:], cs_d[:, :, col0:col0 + T], "cs", [], ["cs"])
            bk = [0]

            def nb():
                b = bk[0] % 8
                bk[0] += 1
                return b

            U, r = load_unit("win_cq")
            for m in range(3):
                b = m
                for k in range(8):
                    mm(PS[b][:], U[:, k, m * 128:(m + 1) * 128], hT[:, k, :], k == 0, k == 7, [r] + HT_ALL, [PB(b)])
                act(sqb[:, m % 2, :], PS[b][:], AF.Square, [PB(b)], [("sqb", m % 2)])
                if m >= 1:
                    mm(PS[7][:], onesb[:], sqb[:, (m - 1) % 2, :], m == 1, False, ["onesb", ("sqb", (m - 1) % 2)], [PB(7)])
            mm(PS[7][:], onesb[:], sqb[:, 0, :], False, True, ["onesb", ("sqb", 0)], [PB(7)])
            rsqrt_act(Rq, PS[7][:], epsc[:, 1:2], [PB(7)], ["Rq"], lnscale=epsc[:, 4:5])
            for m in range(3):
                stt("dve", cqg[:, m, :], PS[m][:], QNG(m), Rq, ALU.mult, ALU.mult, [PB(m), "colsb", "Rq"], ["cqg"])
            U, r = load_unit("win_ckv")
            for m in range(2):
                b = 3 + m
                for k in range(8):
                    mm(PS[b][:], U[:, k, m * 128:(m + 1) * 128], hT[:, k, :], k == 0, k == 7, [r] + HT_ALL, [PB(b)])
                act(sqb[:, m, :], PS[b][:], AF.Square, [PB(b)], [("sqb", m)])
            for m in range(2):
                mm(PS[6][:], onesb[:], sqb[:, m, :], m == 0, m == 1, ["onesb", ("sqb", m)], [PB(6)])
            rsqrt_act(Rkv, PS[6][:], epsc[:, 2:3], [PB(6)], ["Rkv"], lnscale=epsc[:, 5:6])
            for m in range(2):
                stt("dve", ckvg[:, m, :], PS[3 + m][:], KVNG(m), Rkv, ALU.mult, ALU.mult,
                    [PB(3 + m), "colsb", "Rkv"], [("ckvg", m)])
            U, r = load_unit("win_kr")
            b0, b1 = 5, 7
            for k in range(8):
                mm(PS[b0][0:96, :], U[:, k, 0:96], hT[:, k, :], k == 0, k == 7, [r] + HT_ALL, [PB(b0)])
            for k in range(8):
                mm(PS[b1][0:96, :], U[:, k, 96:192], hT[:, k, :], k == 0, k == 7, [r] + HT_ALL, [PB(b1)])
            t1 = tav
            t2 = tbv
            tt("dve", t1[64:96, :], PS[b0][64:96, :], cs[64:96, 0, :], ALU.mult, [PB(b0), "cs"], ["ta"])
            tt("dve", t2[64:96, :], PS[b1][64:96, :], cs[64:96, 1, :], ALU.mult, [PB(b1), "cs"], ["tb"])
            tt("pool", KT[64:96, 0, col0:col0 + T], t1[64:96, :], t2[64:96, :], ALU.add, ["ta", "tb"], [("KTr", t)])
            for hh_ in range(1, 8):
                cp("pool", KT[64:96, hh_, col0:col0 + T], KT[64:96, 0, col0:col0 + T], [("KTr", t)], [("KTr", t)])
            U, r = load_unit("win_qd")
            for m in range(4):
                b = 2 + m
                for k in range(8):
                    mm(PS[b][:], U[:, k, m * 128:(m + 1) * 128], hT[:, k, :], k == 0, k == 7, [r] + HT_ALL, [PB(b)])
                act(QdT[:, m, :], PS[b][:], AF.Copy, [PB(b)], [("QdT", m)], scale=0.125)
            U, r = load_unit("win_kd")
            for m in range(4):
                b = (6 + m) % 8
                for k in range(8):
                    mm(PS[b][:], U[:, k, m * 128:(m + 1) * 128], hT[:, k, :], k == 0, k == 7, [r] + HT_ALL, [PB(b)])
                cp("act" if m % 2 else "dve", KdT[:, m, col0:col0 + T], PS[b][:], [PB(b)], [("KdT", t)])
            U, r = load_unit("wqb")
            Us, rs_ = load_unit("wqbs")
            ta = tav
            tb = tbv
            for hh in range(8):
                bq = 2 * (hh % 4)
                bs = bq + 1
                for k in range(3):
                    mm(PS[bq][0:96, :], U[:, k, hh * 96:(hh + 1) * 96], cqg[:, k, :], k == 0, k == 2, [r, "cqg"], [PB(bq)])
                for k in range(3):
                    mm(PS[bs][0:96, :], Us[:, k, hh * 96:(hh + 1) * 96], cqg[:, k, :], k == 0, k == 2, [rs_, "cqg"], [PB(bs)])
                cp("act", QT[0:64, hh, :], PS[bq][0:64, :], [PB(bq)], [("QT", hh)])
                tt("dve", ta[64:96, :], PS[bq][64:96, :], cs[64:96, 0, :], ALU.mult, [PB(bq), "cs"], ["ta"])
                tt("dve", tb[64:96, :], PS[bs][64:96, :], cs[64:96, 1, :], ALU.mult, [PB(bs), "cs"], ["tb"])
                tt("pool", QT[64:96, hh, :], ta[64:96, :], tb[64:96, :], ALU.add, ["ta", "tb"], [("QT", hh)])
            U, r = load_unit("wkvb_k")
            for hh in range(8):
                b = hh
                for k in range(2):
                    mm(PS[b][0:64, :], U[:, k, hh * 64:(hh + 1) * 64], ckvg[:, k, :], k == 0, k == 1, [r, ("ckvg", k)], [PB(b)])
                cp("act" if hh % 2 else "dve", KT[0:64, hh, col0:col0 + T], PS[b][0:64, :], [PB(b)], [("KTn", t)])
            U, r = load_unit("wkvb_v")
            for sub in range(NSUB):
                b = 4 + sub
                for k in range(2):
                    mm(PS[b][:], ckvg[:, k, sub * 128:(sub + 1) * 128], U[:, k, :], k == 0, k == 1, [r, ("ckvg", k)], [PB(b)])
                j = 4 * t + sub
                cp("dve" if sub % 2 else "act", Vm[:, j, :], PS[b][:], [PB(b)], [("Vm", j)])

            S.alias(["PT0", "PT1", "PT2", "recb", "on0", "sqd"], AR_W_NAMES)
            S.alias([("omT", a) for a in range(4)] + [("odT", a) for a in range(4)], ["RC", "RS", "ta", "tb"])
            nchunk = 4 * t + 4
            KT_ALL = [("KTn", tt_) for tt_ in range(t + 1)] + [("KTr", tt_) for tt_ in range(t + 1)]
            KD_ALL = [("KdT", tt_) for tt_ in range(t + 1)]
            ctr = {"sc": 0, "pt": 0}

            def qlo_of(j):
                return max(0, j - 4 * t) * 128

            items = []

            def mla_item(hh, j):
                accO, accS = (2, 3) if hh % 2 == 0 else (4, 5)
                po = (hh % 2) * 64
                pr = hh // 2
                ql = qlo_of(j)
                diag = j >= 4 * t
                st_ = {}

                def S_():
                    sbk = ctr["sc"] % 2
                    ctr["sc"] += 1
                    st_["sbk"] = sbk
                    mm(PS[sbk][:, ql:T], KT[0:96, hh, j * 128:(j + 1) * 128], QT[0:96, hh, ql:T],
                       True, not diag, KT_ALL + [("QT", hh)], [PB(sbk)])
                    if diag:
                        mm(PS[sbk][:, ql:ql + 128], identb[:], mskb[:], False, True, ["identb", "mskb"], [PB(sbk)])

                def E_():
                    sbk = st_["sbk"]
                    pk = ctr["pt"] % 3
                    ctr["pt"] += 1
                    st_["pk"] = pk
                    act(PT[pk][:, ql:T], PS[sbk][:, ql:T], AF.Exp, [PB(sbk)], [f"PT{pk}"], scale=MLA_SCALE)

                def V_():
                    pk = st_["pk"]
                    mm(PS[accO][:, ql:T], Vm[:, j, pr * 128:(pr + 1) * 128], PT[pk][:, ql:T], j == 0, j == nchunk - 1,
                       [("Vm", j), f"PT{pk}"], [PB(accO)])
                    mm(PS[accS][:, ql:T], onesb[:], PT[pk][:, ql:T], j == 0, j == nchunk - 1,
                       ["onesb", f"PT{pk}"], [PB(accS)])

                def post():
                    recip_act(recb[po:po + 64, :], PS[accS][po:po + 64, :], [PB(accS)], ["recb"])
                    tt("dve", omT[po:po + 64, pr, :], PS[accO][po:po + 64, :], recb[po:po + 64, :], ALU.mult,
                       [PB(accO), "recb"], [("omT", pr)])
                    return None

                return {"S": S_, "E": E_, "V": V_, "post": post if j == nchunk - 1 else None}

            def diff_item(hd, mp, j):
                m = 2 * hd + mp
                po = (m % 2) * 64
                ch = m // 2
                accO, accS = (2, 3) if mp == 0 else (4, 5)
                ql = qlo_of(j)
                jl = j - 4 * t
                near = jl >= -1
                st_ = {}

                def S_():
                    sbk = ctr["sc"] % 2
                    ctr["sc"] += 1
                    st_["sbk"] = sbk
                    mm(PS[sbk][:, ql:T], KdT[po:po + 64, ch, j * 128:(j + 1) * 128], QdT[po:po + 64, ch, ql:T],
                       True, not near, KD_ALL + [("QdT", ch)], [PB(sbk)])
                    if near:
                        blocks = []
                        if jl >= 0:
                            blocks.append((jl, 0))
                        if jl + 1 <= 3:
                            blocks.append((jl + 1, 128))
                        for bi, (qb, toff) in enumerate(blocks):
                            mm(PS[sbk][:, qb * 128:(qb + 1) * 128], identf[:], tz[:, m, toff:toff + 128],
                               False, bi == len(blocks) - 1, ["identf", "tz"], [PB(sbk)])

                def E_():
                    sbk = st_["sbk"]
                    pk = ctr["pt"] % 3
                    ctr["pt"] += 1
                    st_["pk"] = pk
                    nearhi = min(T, (jl + 2) * 128) if near else ql
                    if nearhi > ql:
                        act(PT[pk][:, ql:nearhi], PS[sbk][:, ql:nearhi], AF.Exp, [PB(sbk)], [f"PT{pk}"])
                    if nearhi < T:
                        act(PT[pk][:, nearhi:T], PS[sbk][:, nearhi:T], AF.Exp, [PB(sbk), "colsb"], [f"PT{pk}"],
                            bias=B31(m))

                def V_():
                    pk = st_["pk"]
                    mm(PS[accO][:, ql:T], Vd[:, j, hd * 128:(hd + 1) * 128], PT[pk][:, ql:T], j == 0, j == nchunk - 1,
                       [("Vd", j), f"PT{pk}"], [PB(accO)])
                    mm(PS[accS][:, ql:T], onesb[:], PT[pk][:, ql:T], j == 0, j == nchunk - 1,
                       ["onesb", f"PT{pk}"], [PB(accS)])

                def post():
                    recip_act(recb, PS[accS][:], [PB(accS)], ["recb"])
                    if mp == 0:
                        tt("dve", on[0], PS[accO][:], recb, ALU.mult, [PB(accO), "recb"], ["on0"])
                        return None
                    tt("dve", recb, PS[accO][:], recb, ALU.mult, [PB(accO), "recb"], ["recb"])
                    stt("dve", on[0], recb, nlam, on[0], ALU.mult, ALU.add, ["recb", "nlam", "on0"], ["on0"])
                    tt("dve", sqd, on[0], on[0], ALU.mult, ["on0"], ["sqd"])

                    def deferred():
                        mm(PS[6][:], onesb[:], sqd, True, True, ["onesb", "sqd"], [PB(6)])
                        rsqrt_act(recb, PS[6][:], epsc[:, 3:4], [PB(6)], ["recb"])
                        stt("dve", odT[:, hd, :], on[0], gd2, recb, ALU.mult, ALU.mult, ["on0", "gd2", "recb"], [("odT", hd)])
                    return deferred

                return {"S": S_, "E": E_, "V": V_, "post": post if j == nchunk - 1 else None}

            for hh in range(8):
                for j in range(nchunk):
                    items.append(mla_item(hh, j))
            for hd in range(4):
                for mp in range(2):
                    for j in range(nchunk):
                        items.append(diff_item(hd, mp, j))
            run_pipeline(items)

            S.alias(["g0", "g1", "g2", "g3"], AR_W_NAMES)
            S.alias([("mixT", a) for a in range(8)], [("QT", a) for a in range(8)])
            for half in range(2):
                Ugm, rgm = load_unit(f"wgate_{half}")
                Ugd, rgd = load_unit(f"wgate_{2 + half}")
                Uup, rup = load_unit(f"wup_{half}")
                for dl in range(4):
                    d = half * 4 + dl
                    b0 = 4 * (d % 2)
                    bgm, bgd, bym, byd = b0, b0 + 1, b0 + 2, b0 + 3
                    for k in range(8):
                        mm(PS[bgm][:], Ugm[:, k, dl * 128:(dl + 1) * 128], hT[:, k, :], k == 0, k == 7, [rgm] + HT_ALL, [PB(bgm)])
                    for k in range(8):
                        mm(PS[bgd][:], Ugd[:, k, dl * 128:(dl + 1) * 128], hT[:, k, :], k == 0, k == 7, [rgd] + HT_ALL, [PB(bgd)])
                    for k in range(4):
                        mm(PS[bym][:], Uup[:, k, dl * 128:(dl + 1) * 128], omT[:, k, :], k == 0, k == 3, [rup, ("omT", k)], [PB(bym)])
                    for k in range(4):
                        mm(PS[byd][:], Uup[:, 4 + k, dl * 128:(dl + 1) * 128], odT[:, k, :], k == 0, k == 3, [rup, ("odT", k)], [PB(byd)])
                    act(gbuf[0], PS[bgm][:], AF.Sigmoid, [PB(bgm), "colsb"], ["g0"], bias=BGATE(d))
                    act(gbuf[1], PS[bgd][:], AF.Sigmoid, [PB(bgd), "colsb"], ["g1"], bias=BGATE(8 + d))
                    tt("dve", gbuf[2], gbuf[0], PS[bym][:], ALU.mult, ["g0", PB(bym)], ["g2"])
                    tt("dve", gbuf[3], gbuf[1], PS[byd][:], ALU.mult, ["g1", PB(byd)], ["g3"])
                    tt("pool", mixT[:, d, :], gbuf[2], gbuf[3], ALU.add, ["g2", "g3"], [("mixT", d)])
            down_proj("wo", mixT, "mixT", ln_cb)

        def mem_prep(seq):
            S.alias(["memf", "memb", "memT"], AR_A_NAMES)
            memf = arenaA[:, 0:4096].bitcast(F32).rearrange("p (a t) -> p a t", a=2)
            memb = arenaA[:, 4096:6144].rearrange("p (a t) -> p a t", a=2)
            memT = arenaA[:, 6144:8192].rearrange("p (a t) -> p a t", a=8)
            dma(memf, mem_d[seq * MEM:(seq + 1) * MEM, :].rearrange("(a p) d -> p a d", p=128), "memld", [], ["memf"])
            cp("pool", memb, memf, ["memf"], ["memb"])
            for a in range(2):
                bank = PS[a].bitcast(BF16)
                for k in range(8):
                    S.op("pe", lambda e, k=k, a=a, bank=bank: e.transpose(out=bank[:, k * 128:(k + 1) * 128],
                                                                        in_=memb[:, a, k * 128:(k + 1) * 128], identity=identb[:]),
                         reads=["memb", "identb"], writes=[PB(a)])
                cp("act", memT[:, :, a * 128:(a + 1) * 128], bank.rearrange("p (k t) -> p k t", k=8), [PB(a)], ["memT"])
            for g in range(2):
                U, r = load_unit(f"mkv_{g}")
                for cl in range(4):
                    c = g * 4 + cl
                    b = 2 + (c % 4)
                    for k in range(8):
                        mm(PS[b][:, 0:MEM], U[:, k, cl * 128:(cl + 1) * 128], memT[:, k, :], k == 0, k == 7, [r, "memT"], [PB(b)])
                    cp("act" if c % 2 else "dve", KmT[:, c, :], PS[b][:, 0:MEM], [PB(b)], ["KmT"])
            for g in range(2):
                U, r = load_unit(f"mkv_{2 + g}")
                for a in range(2):
                    b = 6 + a
                    for k in range(8):
                        mm(PS[b][:], memT[:, k, a * 128:(a + 1) * 128], U[:, k, :], k == 0, k == 7, [r, "memT"], [PB(b)])
                    cp("act" if a % 2 else "dve", Vmem[:, a, g * 512:(g + 1) * 512], PS[b][:], [PB(b)], ["Vmem"])

        def mem_attn(ln_cb):
            S.alias([("qmT", a) for a in range(8)] + [("omemT", a) for a in range(8)], AR_A_NAMES)
            S.alias(["PT0", "PT1", "PT2", "recb"], AR_W_NAMES)
            for g in range(2):
                U, r = load_unit(f"mq_{g}")
                for cl in range(4):
                    c = g * 4 + cl
                    b = c % 8
                    for k in range(8):
                        mm(PS[b][:], U[:, k, cl * 128:(cl + 1) * 128], hT[:, k, :], k == 0, k == 7, [r] + HT_ALL, [PB(b)])
                    cp("act" if c % 2 else "dve", qmT[:, c, :], PS[b][:], [PB(b)], [("qmT", c)])
            ctr = {"sc": 0, "pt": 0}

            def mem_item(hd, j):
                a0, a1, aS = (2, 3, 4) if hd % 2 == 0 else (5, 6, 7)
                st_ = {}

                def S_():
                    sbk = ctr["sc"] % 2
                    ctr["sc"] += 1
                    st_["sbk"] = sbk
                    for c in range(2):
                        mm(PS[sbk][:], KmT[:, 2 * hd + c, j * 128:(j + 1) * 128], qmT[:, 2 * hd + c, :], c == 0, c == 1,
                           ["KmT", ("qmT", 2 * hd + c)], [PB(sbk)])

                def E_():
                    pk = ctr["pt"] % 3
                    ctr["pt"] += 1
                    st_["pk"] = pk
                    act(PT[pk], PS[st_["sbk"]][:], AF.Exp, [PB(st_["sbk"])], [f"PT{pk}"], scale=MEM_SCALE)

                def V_():
                    pk = st_["pk"]
                    mm(PS[a0][:], Vmem[:, j, hd * 256:hd * 256 + 128], PT[pk], j == 0, j == 1, ["Vmem", f"PT{pk}"], [PB(a0)])
                    mm(PS[a1][:], Vmem[:, j, hd * 256 + 128:hd * 256 + 256], PT[pk], j == 0, j == 1, ["Vmem", f"PT{pk}"], [PB(a1)])
                    mm(PS[aS][:], onesb[:], PT[pk], j == 0, j == 1, ["onesb", f"PT{pk}"], [PB(aS)])

                def post():
                    recip_act(recb, PS[aS][:], [PB(aS)], ["recb"])
                    tt("dve", omemT[:, 2 * hd, :], PS[a0][:], recb, ALU.mult, [PB(a0), "recb"], [("omemT", 2 * hd)])
                    tt("dve", omemT[:, 2 * hd + 1, :], PS[a1][:], recb, ALU.mult, [PB(a1), "recb"], [("omemT", 2 * hd + 1)])
                    return None

                return {"S": S_, "E": E_, "V": V_, "post": post if j == 1 else None}

            run_pipeline([mem_item(hd, j) for hd in range(4) for j in range(2)])
            down_proj("mo", omemT, "omemT", ln_cb)

        out_tokens = []
        first = True
        ntl = len(tile_rows)
        load_xb(0)
        for sub_ in range(NSUB):
            next_T(sub_)
        hT_ready = {0}
        emit_casts_interleaved(("ffn1",))
        emit_casts(CAST_ORDER[3:9])
        tn = -1
        for seq in range(n_seq):
            for t in range(n_tiles):
                tn += 1
                row0 = seq * SEQ + t * T
                if tn not in hT_ready:
                    load_xb(tn)
                    for sub_ in range(NSUB):
                        next_T(sub_)
                    hT_ready.add(tn)
                dma(h[:], x_d[row0:row0 + T, :].rearrange("(s p) d -> p s d", p=128), "xld",
                    [], [("h", s) for s in range(NSUB)], eng="pool")
                load_lngb(0)
                if first:
                    cast_hook[0] = lambda: emit_casts(CAST_ORDER[9:12], after=[("gT", 0)])
                    cast_hook[1] = lambda: emit_casts_interleaved(("ffn2",), after=[("gT", 21)])
                vd_state = {}

                def vd_after(sub, t=t, vd_state=vd_state):
                    if "U" not in vd_state:
                        vd_state["U"], vd_state["r"] = load_unit("win_vd")
                    U_, r_ = vd_state["U"], vd_state["r"]
                    b = 2 * sub
                    for k in range(8):
                        mm(PS[b][:], hT[:, k, sub * 128:(sub + 1) * 128], U_[:, k, :], k == 0, k == 7,
                           [r_, ("hT", sub)], [PB(b)])
                    cp("dve" if sub % 2 else "act", Vd[:, 4 * t + sub, :], PS[b][:], [PB(b)], [("Vd", 4 * t + sub)])

                ffn("ffn1", LNPipe(C_FFN, stop_stage == 1, row0, 0, after_D=vd_after if stop_stage >= 2 else None))
                cast_hook[0] = cast_hook[1] = None
                first = False
                if stop_stage >= 2:
                    load_lngb(1)
                    mixer(t, t * T, LNPipe(C_MIX, stop_stage == 2, row0, 1))
                if stop_stage >= 3:
                    load_lngb(2)
                    if t == 0:
                        mem_prep(seq)
                    mem_attn(LNPipe(C_MIX, stop_stage == 3, row0, 2))
                if stop_stage >= 4:
                    load_lngb(3)
                    ln4 = LNPipe(C_FFN, True, row0, 3)
                    if tn + 1 < ntl:
                        next_hook[0] = lambda tn=tn: load_xb(tn + 1)
                        ln4.after_A = next_T
                        hT_ready.add(tn + 1)
                    ffn("ffn2", ln4)
        fin = []
        for s in range(NSUB):
            c = S.dma_sems.get(f"out{s}")
            if c:
                fin.append(("d", f"out{s}", c[0]))
        S.wait_all("pool", fin)
        S.emit()
    return nc


def _t5_bucket(n):
    n = np.maximum(n, 0)
    max_exact = 16
    nf = np.maximum(n, 1).astype(np.float32)
    large = max_exact + (np.log(nf / np.float32(max_exact)) / np.float32(math.log(128 / max_exact))
                         * np.float32(32 - max_exact)).astype(np.int32)
    large = np.minimum(large, 31)
    return np.where(n < max_exact, n, large)


def host_consts(inp):
    f32 = np.float32
    c = {}
    c["lnG"] = np.stack([np.broadcast_to(np.asarray(inp[f"ln{i}_g"], f32).reshape(1, D), (128, D)) for i in (1, 2, 3, 4)]).copy()
    c["lnB"] = np.stack([np.broadcast_to(np.asarray(inp[f"ln{i}_b"], f32).reshape(1, D), (128, D)) for i in (1, 2, 3, 4)]).copy()
    cols = np.zeros((128, 96), f32)
    cols[:, 0:3] = np.asarray(inp["q_norm_g"], f32).reshape(3, 128).T
    cols[:, 3:5] = np.asarray(inp["kv_norm_g"], f32).reshape(2, 128).T
    cols[:, 5:21] = np.asarray(inp["b_gate"], f32).reshape(16, 128).T
    cols[:, 21] = np.asarray(inp["diff_norm_g"], f32).reshape(128)
    rb = np.asarray(inp["rel_bias"], f32)
    cols[:, 22:30] = rb[31][None, :]
    for i_ in range(4):
        cols[:, 32 + i_ * 8:40 + i_ * 8] = np.asarray(inp[f"ln{i_ + 1}_g"], f32).reshape(8, 128).T
        cols[:, 64 + i_ * 8:72 + i_ * 8] = np.asarray(inp[f"ln{i_ + 1}_b"], f32).reshape(8, 128).T
    c["cols"] = cols
    c["lamv"] = np.stack([np.broadcast_to(np.asarray(inp[k], f32).reshape(1, 64), (128, 64))
                          for k in ("lam_q1", "lam_k1", "lam_q2", "lam_k2")], axis=1).copy()
    inv_freq = (np.float32(10000.0) ** (-np.arange(0, 32, 2, dtype=f32) / np.float32(32))).astype(f32)
    ang = np.arange(SEQ, dtype=f32)[:, None] * inv_freq[None, :]
    cosv, sinv = np.cos(ang).astype(f32), np.sin(ang).astype(f32)
    cs = np.zeros((128, 2, SEQ), f32)
    cs[64:80, 0] = cosv.T
    cs[80:96, 0] = cosv.T
    cs[64:80, 1] = -sinv.T
    cs[80:96, 1] = sinv.T
    c["cs"] = cs
    kk = np.arange(128)[:, None]
    cc = np.arange(256)[None, :]
    n = cc - kk
    bt = rb[_t5_bucket(n)]
    bt = np.where((n >= 0)[:, :, None], bt, f32(NEG))
    c["tz"] = np.ascontiguousarray(np.transpose(bt, (0, 2, 1))).astype(f32)
    q = np.arange(128)[None, :]
    c["msk"] = np.where(q >= kk, f32(0), f32(NEG)).astype(f32)
    c["idn"] = np.eye(128, dtype=f32)
    return c


def make_in_maps(inp, n_cores, n_seq):
    consts = host_consts(inp)
    wts = {n: np.ascontiguousarray(np.asarray(inp[n], np.float32).reshape(W_SHAPES[n])) for n in W_SHAPES}
    x = np.asarray(inp["x"], np.float32)
    mem = np.asarray(inp["mem"], np.float32)
    maps = []
    for cidx in range(n_cores):
        m = dict(wts)
        m.update(consts)
        m["x"] = np.ascontiguousarray(x[cidx * n_seq:(cidx + 1) * n_seq].reshape(n_seq * SEQ, D))
        m["mem"] = np.ascontiguousarray(mem[cidx * n_seq:(cidx + 1) * n_seq].reshape(n_seq * MEM, D))
        maps.append(m)
    return maps


_NC_CACHE = {}


def kernel(**inputs):
    n_cores, n_seq = 8, 2
    if "full" not in _NC_CACHE:
        _NC_CACHE["full"] = build(n_seq=n_seq, n_tiles=4, stop_stage=4)
    nc = _NC_CACHE["full"]
    maps = make_in_maps(inputs, n_cores, n_seq)
    res = run_bass_kernel_spmd(nc, maps, core_ids=list(range(n_cores)))
    outs = [np.asarray(r["out"]).reshape(n_seq, SEQ, D) for r in res.results]
    return np.concatenate(outs, axis=0).astype(np.float32)
```

```python
import contextlib
import math
import numpy as np
import concourse.bass as bass
import concourse.mybir as mybir
from concourse.bass_utils import run_bass_kernel_spmd

F32 = mybir.dt.float32
BF16 = mybir.dt.bfloat16
AF = mybir.ActivationFunctionType
ALU = mybir.AluOpType

D = 1024
FF = 2816
SEQ = 2048
T = 512
NSUB = 4
MEM = 256
ALPHA = 2.0 ** 0.25
LN_EPS_EFF = 1e-5 / (ALPHA * ALPHA)
RMS_EPS = 1e-6
C_FFN = 0.5 / ALPHA
C_MIX = 1.0 / ALPHA
MLA_SCALE = 96.0 ** -0.5
MEM_SCALE = 256.0 ** -0.5
LAM_INIT = 0.8 - 0.6 * math.exp(-0.3 * 0)
NEG = -30000.0
NRING = 4
UNIT = 4096


class Sched:
    ENGS = ("pe", "act", "dve", "pool", "sp")

    def __init__(self, nc, same_engine_sync=True):
        self.nc = nc
        self.ops = {e: [] for e in self.ENGS}
        self.last_w = {}
        self.rd_e = {}
        self.rd_d = {}
        self.waited = {e: {} for e in self.ENGS}
        self.dma_sems = {}
        self.same_engine_sync = same_engine_sync
        self.marked = {e: set() for e in self.ENGS}

    def _need(self, eng, tok, waits):
        if tok is None:
            return
        if tok[0] == "e":
            _, pe, idx = tok
            if pe == eng and (eng in ("pe", "sp") or not self.same_engine_sync):
                return
            key = ("e", pe)
            if self.waited[eng].get(key, -1) >= idx:
                return
            self.waited[eng][key] = idx
            waits.append(tok)
            self.marked[pe].add(idx)
        else:
            _, sname, val = tok
            key = ("d", sname)
            if self.waited[eng].get(key, -1) >= val:
                return
            self.waited[eng][key] = val
            waits.append(tok)

    def _deps(self, eng, reads, writes):
        waits = []
        for r in reads:
            self._need(eng, self.last_w.get(r), waits)
        for w in writes:
            self._need(eng, self.last_w.get(w), waits)
            for pe, idx in self.rd_e.get(w, {}).items():
                self._need(eng, ("e", pe, idx), waits)
            for t in self.rd_d.get(w, ()):
                self._need(eng, t, waits)
        return waits

    def _commit(self, tok, reads, writes):
        for r in reads:
            if tok[0] == "e":
                d = self.rd_e.setdefault(r, {})
                d[tok[1]] = max(d.get(tok[1], -1), tok[2])
            else:
                self.rd_d.setdefault(r, []).append(tok)
        for w in writes:
            self.last_w[w] = tok
            self.rd_e[w] = {}
            self.rd_d[w] = []

    def op(self, eng, fn, reads=(), writes=()):
        waits = self._deps(eng, reads, writes)
        idx = len(self.ops[eng])
        self.ops[eng].append(("op", fn, waits, None))
        self._commit(("e", eng, idx), reads, writes)

    def dma(self, eng, fn, sem, reads=(), writes=()):
        waits = self._deps(eng, reads, writes)
        c = self.dma_sems.setdefault(sem, [0])
        c[0] += 16
        tok = ("d", sem, c[0])
        self.ops[eng].append(("dma", fn, waits, sem))
        self._commit(tok, reads, writes)
        return tok

    def alias(self, new_names, old_names):
        re, rdm = {}, []
        for o in old_names:
            t = self.last_w.get(o)
            if t is not None:
                if t[0] == "e":
                    re[t[1]] = max(re.get(t[1], -1), t[2])
                else:
                    rdm.append(t)
            for pe, idx in self.rd_e.get(o, {}).items():
                re[pe] = max(re.get(pe, -1), idx)
            rdm.extend(self.rd_d.get(o, ()))
        for n in new_names:
            self.last_w[n] = None
            d = self.rd_e.setdefault(n, {})
            for pe, idx in re.items():
                d[pe] = max(d.get(pe, -1), idx)
            self.rd_d.setdefault(n, []).extend(rdm)

    def wait_all(self, eng, toks):
        waits = []
        for t in toks:
            self._need(eng, t, waits)
        self.ops[eng].append(("wait", None, waits, None))

    def emit(self):
        nc = self.nc
        with contextlib.ExitStack() as st:
            esem = {e: st.enter_context(nc.semaphore("s_" + e)) for e in self.ENGS}
            dsem = {n: st.enter_context(nc.semaphore("d_" + str(n))) for n in self.dma_sems}
            block = st.enter_context(nc.Block())
            cum = {}
            for e in self.ENGS:
                c = 0
                arr = []
                for i in range(len(self.ops[e])):
                    if i in self.marked[e]:
                        c += 1
                    arr.append(c)
                cum[e] = arr

            def run(e, engobj):
                for i, (kind, fn, waits, sem) in enumerate(self.ops[e]):
                    for t in waits:
                        if t[0] == "e":
                            engobj.wait_ge(esem[t[1]], cum[t[1]][t[2]])
                        else:
                            engobj.wait_ge(dsem[t[1]], t[2])
                    if kind == "wait":
                        continue
                    ins = fn(engobj)
                    if kind == "dma":
                        ins.then_inc(dsem[sem], 16)
                    elif i in self.marked[e]:
                        ins.then_inc(esem[e], 1)

            @block.tensor
            def _(eng):
                run("pe", eng)

            @block.scalar
            def _(eng):
                run("act", eng)

            @block.vector
            def _(eng):
                run("dve", eng)

            @block.gpsimd
            def _(eng):
                run("pool", eng)

            @block.sync
            def _(eng):
                run("sp", eng)


W_SHAPES = {
    "ffn1_w1": (D, FF), "ffn1_w3": (D, FF), "ffn1_w2": (FF, D),
    "w_in": (D, 2208), "w_q_b": (384, 768), "w_kv_b": (256, 1024),
    "w_gate": (D, 2 * D), "w_up_mla": (512, D), "w_up_diff": (512, D), "w_o": (D, D),
    "mem_w_q": (D, D), "mem_w_kv": (D, 2 * D), "mem_w_o": (D, D),
    "ffn2_w1": (D, FF), "ffn2_w3": (D, FF), "ffn2_w2": (FF, D),
}


def unit_table():
    units = {}
    order = []

    def add(name, nk, ncols, group, pieces):
        assert nk * ncols <= UNIT, (name, nk, ncols)
        units[name] = (len(order), nk, ncols, group, pieces)
        order.append(name)

    for pre in ("ffn1", "ffn2"):
        for g in range(6):
            c0 = g * 512
            c1 = min(FF, c0 + 512)
            add(f"{pre}_w1_{g}", 8, c1 - c0, pre + "_w1", [(pre + "_w1", 0, 8, (c0, c1), 0)])
            add(f"{pre}_w3_{g}", 8, c1 - c0, pre + "_w3", [(pre + "_w3", 0, 8, (c0, c1), 0)])
        for g in range(6):
            f0 = g * 4
            f1 = min(22, f0 + 4)
            add(f"{pre}_w2_{g}", f1 - f0, 1024, pre + "_w2", [(pre + "_w2", f0, f1, (0, 1024), 0)])
    add("win_cq", 8, 384, "w_in", [("w_in", 0, 8, (0, 384), 0)])
    add("win_ckv", 8, 256, "w_in", [("w_in", 0, 8, (384, 640), 0)])
    add("win_kr", 8, 192, "w_in", [("w_in", 0, 8, (576, 672), 0),
                                    ("w_in", 0, 8, (576, 640), 96),
                                    ("w_in", 0, 8, (656, 672), 160),
                                    ("w_in", 0, 8, (640, 656), 176)])
    add("win_qd", 8, 512, "w_in", [("w_in", 0, 8, (672, 1184), 0)])
    add("win_kd", 8, 512, "w_in", [("w_in", 0, 8, (1184, 1696), 0)])
    add("win_vd", 8, 512, "w_in", [("w_in", 0, 8, (1696, 2208), 0)])
    add("wqb", 3, 768, "w_q_b", [("w_q_b", 0, 3, (0, 768), 0)])
    add("wqbs", 3, 768, "w_q_b", [("ROPESWAP",)])
    add("wkvb_k", 2, 512, "w_kv_b", [("HEADGATHER", "w_kv_b", 0)])
    add("wkvb_v", 2, 512, "w_kv_b", [("HEADGATHER", "w_kv_b", 64)])
    for g in range(4):
        add(f"wgate_{g}", 8, 512, "w_gate", [("w_gate", 0, 8, (g * 512, g * 512 + 512), 0)])
    for g in range(2):
        add(f"wup_{g}", 8, 512, "w_up", [("w_up_mla", 0, 4, (g * 512, g * 512 + 512), 0),
                                          ("w_up_diff", 0, 4, (g * 512, g * 512 + 512), 4 * 512)])
    for g in range(2):
        add(f"wo_{g}", 4, 1024, "w_o", [("w_o", g * 4, g * 4 + 4, (0, 1024), 0)])
    for g in range(2):
        add(f"mq_{g}", 8, 512, "mem_w_q", [("mem_w_q", 0, 8, (g * 512, g * 512 + 512), 0)])
    for g in range(4):
        add(f"mkv_{g}", 8, 512, "mem_w_kv", [("mem_w_kv", 0, 8, (g * 512, g * 512 + 512), 0)])
    for g in range(2):
        add(f"mo_{g}", 4, 1024, "mem_w_o", [("mem_w_o", g * 4, g * 4 + 4, (0, 1024), 0)])
    return units, order


def build(n_seq=2, n_tiles=4, stop_stage=4):
    nc = bass.Bass("TRN2", target_bir_lowering=False)
    ntok = n_seq * SEQ
    x_d = nc.dram_tensor("x", [ntok, D], F32, kind="ExternalInput").ap()
    mem_d = nc.dram_tensor("mem", [n_seq * MEM, D], F32, kind="ExternalInput").ap()
    out_d = nc.dram_tensor("out", [ntok, D], F32, kind="ExternalOutput").ap()
    wd = {n: nc.dram_tensor(n, list(s), F32, kind="ExternalInput").ap() for n, s in W_SHAPES.items()}
    lng_d = nc.dram_tensor("lnG", [4, 128, D], F32, kind="ExternalInput").ap()
    lnb_d = nc.dram_tensor("lnB", [4, 128, D], F32, kind="ExternalInput").ap()
    cols_d = nc.dram_tensor("cols", [128, 96], F32, kind="ExternalInput").ap()
    lamv_d = nc.dram_tensor("lamv", [128, 4, 64], F32, kind="ExternalInput").ap()
    cs_d = nc.dram_tensor("cs", [128, 2, SEQ], F32, kind="ExternalInput").ap()
    tz_d = nc.dram_tensor("tz", [128, 8, 256], F32, kind="ExternalInput").ap()
    msk_d = nc.dram_tensor("msk", [128, 128], F32, kind="ExternalInput").ap()
    idn_d = nc.dram_tensor("idn", [128, 128], F32, kind="ExternalInput").ap()
    units, uorder = unit_table()
    NU = len(uorder)
    scr = nc.dram_tensor("scr", [NU, 128, UNIT], BF16, kind="Internal").ap()

    S = Sched(nc)
    st = contextlib.ExitStack()
    with st:
        def sb(name, shape, dt):
            return st.enter_context(nc.sbuf_tensor("sb_" + name, shape, dt))

        h = sb("h", [128, NSUB, D], F32)
        hT = sb("hT", [128, 8, T], BF16)
        KT = sb("KT", [128, 8, SEQ], BF16)
        Vm = sb("Vm", [128, 16, 512], BF16)
        KdT = sb("KdT", [128, 4, SEQ], BF16)
        Vd = sb("Vd", [128, 16, 512], BF16)
        KmT = sb("KmT", [128, 8, MEM], BF16)
        Vmem = sb("Vmem", [128, 2, D], BF16)
        arenaA = sb("arenaA", [128, 12288], BF16)
        arenaW = sb("arenaW", [128, 2048], F32)
        ring = sb("ring", [128, NRING, UNIT], BF16)
        lnGB = sb("lnGB", [128, 2, D], F32)
        cs = sb("cs", [128, 2, T], F32)
        tzh = sb("tzh", [128, 8, 256], BF16)
        tzl = sb("tzl", [128, 8, 256], BF16)
        colsb = sb("colsb", [128, 96], F32)
        lamv = sb("lamv", [128, 4, 64], F32)
        lamt = sb("lamt", [128, 64], F32)
        small = sb("small", [128, 32], F32)
        identf = sb("identf", [128, 128], F32)
        identb = sb("identb", [128, 128], BF16)
        onesb = sb("onesb", [128, 128], BF16)
        mskf = sb("mskf", [128, 128], F32)
        mskb = sb("mskb", [128, 128], BF16)
        epsc = sb("epsc", [128, 8], F32)
        st6 = sb("st6", [128, 2, 2, 6], F32)
        mv = sb("mv", [128, 2, 8], F32)

        PS = [st.enter_context(nc.psum_tensor(f"ps{i}", [128, 512], F32)) for i in range(8)]

        def PB(i):
            return ("ps", i)

        gT = arenaA[:, 0:22 * T].rearrange("p (f t) -> p f t", f=22)
        QT = arenaA[:, 0:4096].rearrange("p (a t) -> p a t", a=8)
        QdT = arenaA[:, 4096:6144].rearrange("p (a t) -> p a t", a=4)
        omT = arenaA[:, 8192:10240].rearrange("p (a t) -> p a t", a=4)
        odT = arenaA[:, 10240:12288].rearrange("p (a t) -> p a t", a=4)
        mixT = arenaA[:, 0:4096].rearrange("p (a t) -> p a t", a=8)
        qmT = arenaA[:, 4096:8192].rearrange("p (a t) -> p a t", a=8)
        omemT = arenaA[:, 8192:12288].rearrange("p (a t) -> p a t", a=8)
        cqg = arenaA[:, 6144:7680].rearrange("p (a t) -> p a t", a=3)
        RCv = arenaA[:, 8192:9216].bitcast(F32)
        RSv = arenaA[:, 9216:10240].bitcast(F32)
        tav = arenaA[:, 10240:11264].bitcast(F32)
        tbv = arenaA[:, 11264:12288].bitcast(F32)
        sbuf2 = [arenaW[:, 0:512], arenaW[:, 512:1024]]
        ckvg = arenaW[:, 1024:1536].bitcast(BF16).rearrange("p (a t) -> p a t", a=2)
        sqb = arenaW[:, 1536:2048].bitcast(BF16).rearrange("p (a t) -> p a t", a=2)
        Rq = arenaW[:, 0:512]
        Rkv = arenaW[:, 512:1024]
        PT = [arenaW[:, 0:256].bitcast(BF16), arenaW[:, 256:512].bitcast(BF16), arenaW[:, 512:768].bitcast(BF16)]
        recb = arenaW[:, 768:1280]
        on = [arenaW[:, 1280:1792]]
        sqd = arenaW[:, 1792:2048].bitcast(BF16)
        gbuf = [arenaW[:, i * 512:(i + 1) * 512] for i in range(4)]

        def mm(out, lhsT, rhs, start, stop, reads, writes, skip=False):
            if skip:
                S.op("pe", lambda e: e.matmul(out, lhsT=lhsT, rhs=rhs, start=start, stop=stop, skip_group_check=True),
                     reads=reads, writes=writes)
            else:
                S.op("pe", lambda e: e.matmul(out, lhsT=lhsT, rhs=rhs, start=start, stop=stop),
                     reads=reads, writes=writes)

        def act(out, in_, func, reads, writes, bias=None, scale=None):
            kw = {}
            if bias is not None:
                kw["bias"] = bias
            if scale is not None:
                kw["scale"] = scale
            S.op("act", lambda e: e.activation(out=out, in_=in_, func=func, **kw), reads=reads, writes=writes)

        def tt(eng, out, in0, in1, op, reads, writes):
            S.op(eng, lambda e: e.tensor_tensor(out=out, in0=in0, in1=in1, op=op), reads=reads, writes=writes)

        def stt(eng, out, in0, scalar, in1, op0, op1, reads, writes):
            S.op(eng, lambda e: e.scalar_tensor_tensor(out=out, in0=in0, scalar=scalar, in1=in1, op0=op0, op1=op1),
                 reads=reads, writes=writes)

        def ts(eng, out, in0, s1, s2, op0, op1, reads, writes):
            if s2 is None:
                S.op(eng, lambda e: e.tensor_scalar(out=out, in0=in0, scalar1=s1, scalar2=None, op0=op0),
                     reads=reads, writes=writes)
            else:
                S.op(eng, lambda e: e.tensor_scalar(out=out, in0=in0, scalar1=s1, scalar2=s2, op0=op0, op1=op1),
                     reads=reads, writes=writes)

        def rsqrt(out, in_, epscol, reads, writes):
            act(out, in_, AF.Sqrt, list(reads) + ["epsc"], list(writes), bias=epscol)
            S.op("dve", lambda e: e.reciprocal(out=out, in_=out), reads=list(writes), writes=list(writes))

        def recip_act(out, in_, reads, writes):
            act(out, in_, AF.Ln, list(reads), list(writes))
            act(out, out, AF.Exp, list(writes), list(writes), scale=-1.0)

        def rsqrt_act(out, in_, epscol, reads, writes, lnscale=None):
            act(out, in_, AF.Ln, list(reads) + ["epsc"], list(writes), bias=epscol)
            if lnscale is None:
                act(out, out, AF.Exp, list(writes), list(writes), scale=-0.5)
            else:
                act(out, out, AF.Exp, list(writes) + ["epsc"], list(writes), scale=-0.5, bias=lnscale)

        def cp(eng, out, in_, reads, writes):
            if eng == "act":
                S.op("act", lambda e: e.copy(out=out, in_=in_), reads=reads, writes=writes)
            else:
                S.op(eng, lambda e: e.tensor_copy(out=out, in_=in_), reads=reads, writes=writes)

        def dma(out, in_, sem, reads, writes, eng="sp"):
            return S.dma(eng, lambda e: e.dma_start(out=out, in_=in_), sem, reads=reads, writes=writes)

        group_units = {}
        for name in uorder:
            group_units.setdefault(units[name][3], []).append(name)

        def emit_casts(groups, after=()):
            after = list(after)
            for group in groups:
                for name in group_units[group]:
                    idx, nk, ncols, _, pieces = units[name]
                    dst3 = scr[idx, :, 0:nk * ncols].rearrange("p (k c) -> p k c", k=nk)
                    sem = f"cu{idx}"
                    wr = [("scr", name)]
                    for pc in pieces:
                        if pc[0] == "ROPESWAP":
                            w = wd["w_q_b"].rearrange("(k p) (hh c) -> p k hh c", p=128, c=96)
                            d4 = scr[idx, :, 0:nk * ncols].rearrange("p (k hh c) -> p k hh c", k=3, c=96)
                            for kk in range(3):
                                dma(d4[:, kk, :, 0:64], w[:, kk, :, 0:64], sem, after, wr, eng="pool")
                                dma(d4[:, kk, :, 64:80], w[:, kk, :, 80:96], sem, after, wr, eng="pool")
                                dma(d4[:, kk, :, 80:96], w[:, kk, :, 64:80], sem, after, wr, eng="pool")
                        elif pc[0] == "HEADGATHER":
                            w = wd[pc[1]].rearrange("(k p) (hh c) -> p k hh c", p=128, c=128)
                            d4 = scr[idx, :, 0:nk * ncols].rearrange("p (k hh c) -> p k hh c", k=2, c=64)
                            for kk in range(2):
                                dma(d4[:, kk, :, :], w[:, kk, :, pc[2]:pc[2] + 64], sem, after, wr, eng="pool")
                        else:
                            wname, k0, k1, (c0, c1), dc0 = pc
                            src = wd[wname][k0 * 128:k1 * 128, c0:c1].rearrange("(k p) c -> p k c", p=128)
                            if len(pieces) == 1:
                                dst = dst3
                            elif name.startswith("wup_"):
                                kb = dc0 // 512
                                dst = dst3[:, kb:kb + (k1 - k0), :]
                            else:
                                dst = dst3[:, :, dc0:dc0 + (c1 - c0)]
                            dma(dst, src, sem, after, wr, eng="pool")
                for n in group_units[group]:
                    sem = f"cu{units[n][0]}"
                    S.last_w[("scr", n)] = ("d", sem, S.dma_sems[sem][0])

        CAST_ORDER = ["ffn1_w1", "ffn1_w3", "ffn1_w2", "w_in", "w_q_b", "w_kv_b", "w_gate", "w_up", "w_o",
                      "mem_w_kv", "mem_w_q", "mem_w_o", "ffn2_w1", "ffn2_w3", "ffn2_w2"]

        ring_ctr = [0]

        def load_unit(name):
            idx, nk, ncols, group, _ = units[name]
            slot = ring_ctr[0] % NRING
            ring_ctr[0] += 1
            n = nk * ncols
            dma(ring[:, slot, 0:n], scr[idx, :, 0:n], f"ring{slot}", [("scr", name)], [("ring", slot)])
            return ring[:, slot, 0:n].rearrange("p (k c) -> p k c", k=nk), ("ring", slot)

        dma(colsb[:], cols_d, "c_cols", [], ["colsb"])
        dma(lamv[:], lamv_d, "c_lamv", [], ["lamv"])
        tzf = arenaA[:, 0:4096].bitcast(F32).rearrange("p (a t) -> p a t", a=8)
        tzt = arenaA[:, 4096:8192].bitcast(F32).rearrange("p (a t) -> p a t", a=8)
        dma(tzf, tz_d, "c_tz", [], ["tzf"])
        cp("dve", tzh[:], tzf, ["tzf"], ["tzh"])
        cp("dve", tzt, tzh[:], ["tzh"], ["tzt"])
        tt("dve", tzf, tzf, tzt, ALU.subtract, ["tzf", "tzt"], ["tzf"])
        cp("dve", tzl[:], tzf, ["tzf"], ["tzl"])
        dma(mskf[:], msk_d, "c_msk", [], ["mskf"])
        dma(identf[:], idn_d, "c_idn", [], ["identf"])
        cp("dve", identb[:], identf[:], ["identf"], ["identb"])
        cp("dve", mskb[:], mskf[:], ["mskf"], ["mskb"])
        S.op("dve", lambda e: e.memset(onesb[:], 1.0), writes=["onesb"])
        for i_, v_ in enumerate((LN_EPS_EFF, 384.0 * RMS_EPS, 256.0 * RMS_EPS, 128.0 * RMS_EPS,
                                 0.5 * math.log(384.0), 0.5 * math.log(256.0))):
            S.op("dve", lambda e, i_=i_, v_=v_: e.memset(epsc[:, i_:i_ + 1], v_), writes=["epsc"])
        tt("dve", lamt[:], lamv[:, 0, :], lamv[:, 1, :], ALU.mult, ["lamv"], ["lamt"])
        S.op("dve", lambda e: e.reduce_sum(out=small[:, 0:1], in_=lamt[:], axis=mybir.AxisListType.X),
             reads=["lamt"], writes=["small0"])
        tt("dve", lamt[:], lamv[:, 2, :], lamv[:, 3, :], ALU.mult, ["lamv"], ["lamt"])
        S.op("dve", lambda e: e.reduce_sum(out=small[:, 1:2], in_=lamt[:], axis=mybir.AxisListType.X),
             reads=["lamt"], writes=["small1"])
        act(small[:, 2:4], small[:, 0:2], AF.Exp, ["small0", "small1"], ["small23"])
        tt("dve", small[:, 4:5], small[:, 2:3], small[:, 3:4], ALU.subtract, ["small23"], ["small4"])
        ts("dve", small[:, 4:5], small[:, 4:5], LAM_INIT, None, ALU.add, ALU.bypass, ["small4"], ["small4"])
        ts("dve", small[:, 5:6], small[:, 4:5], -1.0, None, ALU.mult, ALU.bypass, ["small4"], ["nlam"])
        ts("dve", small[:, 6:7], colsb[:, 21:22], (1.0 - LAM_INIT) * math.sqrt(128.0), None, ALU.mult, ALU.bypass,
           ["colsb"], ["gd2"])
        nlam = small[:, 5:6]
        gd2 = small[:, 6:7]
        QNG = lambda m: colsb[:, m:m + 1]
        KVNG = lambda m: colsb[:, 3 + m:4 + m]
        BGATE = lambda c: colsb[:, 5 + c:6 + c]
        B31 = lambda m: colsb[:, 22 + m:23 + m]

        def run_pipeline(items, L=1, Dd=2):
            deferred = []
            n = len(items)
            for i in range(n + L):
                if i < n:
                    items[i]["S"]()
                    items[i]["E"]()
                if i >= L:
                    it = items[i - L]
                    it["V"]()
                    if it.get("post"):
                        dfn = it["post"]()
                        if dfn is not None:
                            deferred.append((i + Dd, dfn))
                due = [d for d in deferred if d[0] <= i]
                deferred = [d for d in deferred if d[0] > i]
                for _, fn in due:
                    fn()
            for _, fn in deferred:
                fn()

        HT_ALL = [("hT", s) for s in range(NSUB)]
        tr_ctr = [0]

        def make_hT(sub, bank0, ln_idx=None):
            for k in range(8):
                b = bank0 + k // 4
                S.op("pe", lambda e, k=k, b=b: e.transpose(out=PS[b][:, (k % 4) * 128:(k % 4 + 1) * 128],
                                                           in_=h[:, sub, k * 128:(k + 1) * 128], identity=identf[:]),
                     reads=[("h", sub), "identf"], writes=[PB(b)])
            for k in range(8):
                b = bank0 + k // 4
                src = PS[b][:, (k % 4) * 128:(k % 4 + 1) * 128]
                dst = hT[:, k, sub * 128:(sub + 1) * 128]
                if ln_idx is None:
                    cp("act" if k % 2 else "dve", dst, src, [PB(b)], [("hT", sub)])
                else:
                    gc = colsb[:, 32 + ln_idx * 8 + k:33 + ln_idx * 8 + k]
                    bc = colsb[:, 64 + ln_idx * 8 + k:65 + ln_idx * 8 + k]
                    if k % 2:
                        act(dst, src, AF.Identity, [PB(b), "colsb"], [("hT", sub)], bias=bc, scale=gc)
                    else:
                        ts("dve", dst, src, gc, bc, ALU.mult, ALU.add, [PB(b), "colsb"], [("hT", sub)])

        def load_lngb(i):
            dma(lnGB[:, 0, :], lng_d[i], "lnG", [], ["lnG"], eng="pool")
            dma(lnGB[:, 1, :], lnb_d[i], "lnB", [], ["lnB"], eng="pool")

        class LNPipe:
            def __init__(self, c, final, tile_row0, ln_idx, after_D=None):
                self.c, self.final, self.row0, self.i, self.ln_idx = c, final, tile_row0, 0, ln_idx
                self.after_D = after_D
                self.after_A = None

            def A(self, sub):
                hs = ("h", sub)
                c = self.c
                for half in range(2):
                    b = 2 * sub + half
                    stt("dve", h[:, sub, half * 512:(half + 1) * 512], PS[b][:], c, h[:, sub, half * 512:(half + 1) * 512],
                        ALU.mult, ALU.add, [PB(b), hs], [hs])
                k = sub % 2
                for half in range(2):
                    S.op("dve", lambda e, half=half: e.bn_stats(out=st6[:, k, half, :], in_=h[:, sub, half * 512:(half + 1) * 512]),
                         reads=[hs], writes=[("st6", k)])
                S.op("dve", lambda e: e.bn_aggr(out=mv[:, k, 0:2], in_=st6[:, k, :, :]), reads=[("st6", k)], writes=[("mv", k)])

            def hookA(self, sub):
                if self.after_A is not None:
                    self.after_A(sub)

            def A2(self, sub):
                hs = ("h", sub)
                k = sub % 2
                mk = [("mv", k)]
                act(mv[:, k, 2:3], mv[:, k, 1:2], AF.Ln, mk + ["epsc"], [("mvr", k)], bias=epsc[:, 0:1])
                act(mv[:, k, 2:3], mv[:, k, 2:3], AF.Exp, [("mvr", k)], [("mvr", k)], scale=-0.5)
                S.op("act", lambda e: e.mul(out=mv[:, k, 3:4], in_=mv[:, k, 2:3], mul=-1.0), reads=[("mvr", k)], writes=[("mvn", k)])
                act(mv[:, k, 4:5], mv[:, k, 0:1], AF.Copy, mk + [("mvn", k)], [("mvb", k)], scale=mv[:, k, 3:4])
                act(h[:, sub, :], h[:, sub, :], AF.Identity, [hs, ("mvr", k), ("mvb", k)], [hs], bias=mv[:, k, 4:5], scale=mv[:, k, 2:3])

            def B(self, sub):
                hs = ("h", sub)
                tt("pool", h[:, sub, :], h[:, sub, :], lnGB[:, 0, :], ALU.mult, [hs, "lnG"], [hs])
                tt("pool", h[:, sub, :], h[:, sub, :], lnGB[:, 1, :], ALU.add, [hs, "lnB"], [hs])

            def C(self, sub):
                r0 = self.row0 + sub * 128
                dma(out_d[r0:r0 + 128, :], h[:, sub, :], f"out{sub}", [("h", sub)], [("outd", sub)], eng="pool")

            def D(self, sub):
                make_hT(sub, 2 * sub, self.ln_idx)
                if self.after_D is not None:
                    self.after_D(sub)

            def step(self):
                i = self.i
                self.i += 1
                if self.final:
                    order = [(0, self.A), (0, self.hookA), (0, self.A2), (1, self.B), (2, self.C)]
                else:
                    order = [(0, self.A), (2, self.D), (0, self.A2), (3, self.B)]
                for k, fn in order:
                    sub = i - k
                    if 0 <= sub < NSUB:
                        fn(sub)

            def flush(self):
                while self.i < NSUB + 3:
                    self.step()

        def down_proj(uname, actT, actname, ln_cb):
            U0, r0 = load_unit(f"{uname}_0")
            U1, r1 = load_unit(f"{uname}_1")
            for sub in range(NSUB):
                for k in range(8):
                    U_, r_ = (U0, r0) if k < 4 else (U1, r1)
                    for half in range(2):
                        b = 2 * sub + half
                        mm(PS[b][:], actT[:, k, sub * 128:(sub + 1) * 128], U_[:, k % 4, half * 512:(half + 1) * 512],
                           k == 0, k == 7, [(actname, k), r_], [PB(b)])
                ln_cb.step()
            ln_cb.flush()

        def cast_one(name, after=()):
            g = units[name][3]
            saved = group_units[g]
            group_units[g] = [name]
            emit_casts([g], after=after)
            group_units[g] = saved

        def emit_casts_interleaved(pres, after=()):
            for pre in pres:
                for g in range(6):
                    cast_one(f"{pre}_w1_{g}", after)
                    cast_one(f"{pre}_w3_{g}", after)
                for g in range(6):
                    cast_one(f"{pre}_w2_{g}", after)

        tile_rows = [seq_ * SEQ + t_ * T for seq_ in range(n_seq) for t_ in range(n_tiles)]

        xb = arenaW[:].bitcast(BF16).rearrange("p (a t) -> p a t", a=4)

        def load_xb(n):
            r0 = tile_rows[n]
            S.alias(["xb"], AR_W_NAMES)
            dma(xb, x_d[r0:r0 + T, :].rearrange("(s p) d -> p s d", p=128), "xb", [], ["xb"], eng="pool")

        def next_T(sub):
            bank = PS[2 * sub].bitcast(BF16)
            for k in range(8):
                S.op("pe", lambda e, k=k: e.transpose(out=bank[:, k * 128:(k + 1) * 128],
                                                      in_=xb[:, sub, k * 128:(k + 1) * 128], identity=identb[:]),
                     reads=["xb", "identb"], writes=[PB(2 * sub)])
            cp("act" if sub % 2 else "dve", hT[:, :, sub * 128:(sub + 1) * 128],
               bank.rearrange("p (k t) -> p k t", k=8), [PB(2 * sub)], [("hT", sub)])

        next_hook = [None]
        cast_hook = [None, None]

        def ffn(pre, ln_cb):
            S.alias([("gT", f) for f in range(22)], ["arenaA_all"] + AR_A_NAMES)
            S.alias(["silu0", "silu1"], AR_W_NAMES)
            pair = 0
            for g in range(6):
                ncol = 512 if g < 5 else 256
                U1, r1 = load_unit(f"{pre}_w1_{g}")
                U3, r3 = load_unit(f"{pre}_w3_{g}")
                for fl in range(ncol // 128):
                    f = g * 4 + fl
                    ba, bb = 2 * (pair % 4), 2 * (pair % 4) + 1
                    pair += 1
                    for k in range(8):
                        mm(PS[ba][:], U1[:, k, fl * 128:(fl + 1) * 128], hT[:, k, :], k == 0, k == 7,
                           [r1] + HT_ALL, [PB(ba)])
                    for k in range(8):
                        mm(PS[bb][:], U3[:, k, fl * 128:(fl + 1) * 128], hT[:, k, :], k == 0, k == 7,
                           [r3] + HT_ALL, [PB(bb)])
                    sname = f"silu{f % 2}"
                    act(sbuf2[f % 2], PS[ba][:], AF.Silu, [PB(ba)], [sname])
                    tt("dve", gT[:, f, :], sbuf2[f % 2], PS[bb][:], ALU.mult, [sname, PB(bb)], [("gT", f)])
                    if f == 0 and cast_hook[0] is not None:
                        cast_hook[0]()
                    if f == 21 and cast_hook[1] is not None:
                        cast_hook[1]()
            if next_hook[0] is not None:
                next_hook[0]()
                next_hook[0] = None
            for g in range(3):
                U2, r2 = load_unit(f"{pre}_w2_{g}")
                for fl in range(4):
                    f = g * 4 + fl
                    for sub in range(NSUB):
                        for half in range(2):
                            b = 2 * sub + half
                            mm(PS[b][:], gT[:, f, sub * 128:(sub + 1) * 128], U2[:, fl, half * 512:(half + 1) * 512],
                               f == 0, False, [("gT", f), r2], [PB(b)])
            U2a, r2a = load_unit(f"{pre}_w2_3")
            U2b, r2b = load_unit(f"{pre}_w2_4")
            U2c, r2c = load_unit(f"{pre}_w2_5")
            tail = ([(U2a, r2a, fl, 12 + fl) for fl in range(4)] + [(U2b, r2b, fl, 16 + fl) for fl in range(4)]
                    + [(U2c, r2c, fl, 20 + fl) for fl in range(2)])
            for sub in range(NSUB):
                for (U2, r2, fl, f) in tail:
                    for half in range(2):
                        b = 2 * sub + half
                        mm(PS[b][:], gT[:, f, sub * 128:(sub + 1) * 128], U2[:, fl, half * 512:(half + 1) * 512],
                           False, f == 21, [("gT", f), r2], [PB(b)])
                ln_cb.step()
            ln_cb.flush()

        AR_A_NAMES = ([("gT", f) for f in range(22)] + [("QT", a) for a in range(8)] + ["cqg"]
                      + [("QdT", a) for a in range(4)] + [("omT", a) for a in range(4)] + [("odT", a) for a in range(4)]
                      + [("mixT", a) for a in range(8)] + [("qmT", a) for a in range(8)] + [("omemT", a) for a in range(8)]
                      + ["memf", "memb", "memT", "RC", "RS", "ta", "tb", "tzf", "tzt"])
        AR_W_NAMES = ["xb", "silu0", "silu1", ("ckvg", 0), ("ckvg", 1), ("sqb", 0), ("sqb", 1), "Rq", "Rkv",
                      "PT0", "PT1", "PT2", "recb", "on0", "sqd",
                      "g0", "g1", "g2", "g3"]

        def mixer(t, col0, ln_cb):
            S.alias([("QT", a) for a in range(8)] + [("QdT", a) for a in range(4)]
                    + ["RC", "RS", "ta", "tb", "cqg"], AR_A_NAMES)
            S.alias([("ckvg", 0), ("ckvg", 1), ("sqb", 0), ("sqb", 1), "Rq", "Rkv"], AR_W_NAMES)
            dma(cs[:], cs_d[:, :, col0:col0 + T], "cs", [], ["cs"])
            bk = [0]

            def nb():
                b = bk[0] % 8
                bk[0] += 1
                return b

            U, r = load_unit("win_cq")
            for m in range(3):
                b = m
                for k in range(8):
                    mm(PS[b][:], U[:, k, m * 128:(m + 1) * 128], hT[:, k, :], k == 0, k == 7, [r] + HT_ALL, [PB(b)])
                act(sqb[:, m % 2, :], PS[b][:], AF.Square, [PB(b)], [("sqb", m % 2)])
                if m >= 1:
                    mm(PS[7][:], onesb[:], sqb[:, (m - 1) % 2, :], m == 1, False, ["onesb", ("sqb", (m - 1) % 2)], [PB(7)])
            mm(PS[7][:], onesb[:], sqb[:, 0, :], False, True, ["onesb", ("sqb", 0)], [PB(7)])
            rsqrt_act(Rq, PS[7][:], epsc[:, 1:2], [PB(7)], ["Rq"], lnscale=epsc[:, 4:5])
            for m in range(3):
                stt("dve", cqg[:, m, :], PS[m][:], QNG(m), Rq, ALU.mult, ALU.mult, [PB(m), "colsb", "Rq"], ["cqg"])
            U, r = load_unit("win_ckv")
            for m in range(2):
                b = 3 + m
                for k in range(8):
                    mm(PS[b][:], U[:, k, m * 128:(m + 1) * 128], hT[:, k, :], k == 0, k == 7, [r] + HT_ALL, [PB(b)])
                act(sqb[:, m, :], PS[b][:], AF.Square, [PB(b)], [("sqb", m)])
            for m in range(2):
                mm(PS[6][:], onesb[:], sqb[:, m, :], m == 0, m == 1, ["onesb", ("sqb", m)], [PB(6)])
            rsqrt_act(Rkv, PS[6][:], epsc[:, 2:3], [PB(6)], ["Rkv"], lnscale=epsc[:, 5:6])
            for m in range(2):
                stt("dve", ckvg[:, m, :], PS[3 + m][:], KVNG(m), Rkv, ALU.mult, ALU.mult,
                    [PB(3 + m), "colsb", "Rkv"], [("ckvg", m)])
            U, r = load_unit("win_kr")
            b0, b1 = 5, 7
            for k in range(8):
                mm(PS[b0][0:96, :], U[:, k, 0:96], hT[:, k, :], k == 0, k == 7, [r] + HT_ALL, [PB(b0)])
            for k in range(8):
                mm(PS[b1][0:96, :], U[:, k, 96:192], hT[:, k, :], k == 0, k == 7, [r] + HT_ALL, [PB(b1)])
            t1 = tav
            t2 = tbv
            tt("dve", t1[64:96, :], PS[b0][64:96, :], cs[64:96, 0, :], ALU.mult, [PB(b0), "cs"], ["ta"])
            tt("dve", t2[64:96, :], PS[b1][64:96, :], cs[64:96, 1, :], ALU.mult, [PB(b1), "cs"], ["tb"])
            tt("pool", KT[64:96, 0, col0:col0 + T], t1[64:96, :], t2[64:96, :], ALU.add, ["ta", "tb"], [("KTr", t)])
            for hh_ in range(1, 8):
                cp("pool", KT[64:96, hh_, col0:col0 + T], KT[64:96, 0, col0:col0 + T], [("KTr", t)], [("KTr", t)])
            U, r = load_unit("win_qd")
            for m in range(4):
                b = 2 + m
                for k in range(8):
                    mm(PS[b][:], U[:, k, m * 128:(m + 1) * 128], hT[:, k, :], k == 0, k == 7, [r] + HT_ALL, [PB(b)])
                act(QdT[:, m, :], PS[b][:], AF.Copy, [PB(b)], [("QdT", m)], scale=0.125)
            U, r = load_unit("win_kd")
            for m in range(4):
                b = (6 + m) % 8
                for k in range(8):
                    mm(PS[b][:], U[:, k, m * 128:(m + 1) * 128], hT[:, k, :], k == 0, k == 7, [r] + HT_ALL, [PB(b)])
                cp("act" if m % 2 else "dve", KdT[:, m, col0:col0 + T], PS[b][:], [PB(b)], [("KdT", t)])
            U, r = load_unit("wqb")
            Us, rs_ = load_unit("wqbs")
            ta = tav
            tb = tbv
            for hh in range(8):
                bq = 2 * (hh % 4)
                bs = bq + 1
                for k in range(3):
                    mm(PS[bq][0:96, :], U[:, k, hh * 96:(hh + 1) * 96], cqg[:, k, :], k == 0, k == 2, [r, "cqg"], [PB(bq)])
                for k in range(3):
                    mm(PS[bs][0:96, :], Us[:, k, hh * 96:(hh + 1) * 96], cqg[:, k, :], k == 0, k == 2, [rs_, "cqg"], [PB(bs)])
                cp("act", QT[0:64, hh, :], PS[bq][0:64, :], [PB(bq)], [("QT", hh)])
                tt("dve", ta[64:96, :], PS[bq][64:96, :], cs[64:96, 0, :], ALU.mult, [PB(bq), "cs"], ["ta"])
                tt("dve", tb[64:96, :], PS[bs][64:96, :], cs[64:96, 1, :], ALU.mult, [PB(bs), "cs"], ["tb"])
                tt("pool", QT[64:96, hh, :], ta[64:96, :], tb[64:96, :], ALU.add, ["ta", "tb"], [("QT", hh)])
            U, r = load_unit("wkvb_k")
            for hh in range(8):
                b = hh
                for k in range(2):
                    mm(PS[b][0:64, :], U[:, k, hh * 64:(hh + 1) * 64], ckvg[:, k, :], k == 0, k == 1, [r, ("ckvg", k)], [PB(b)])
                cp("act" if hh % 2 else "dve", KT[0:64, hh, col0:col0 + T], PS[b][0:64, :], [PB(b)], [("KTn", t)])
            U, r = load_unit("wkvb_v")
            for sub in range(NSUB):
                b = 4 + sub
                for k in range(2):
                    mm(PS[b][:], ckvg[:, k, sub * 128:(sub + 1) * 128], U[:, k, :], k == 0, k == 1, [r, ("ckvg", k)], [PB(b)])
                j = 4 * t + sub
                cp("dve" if sub % 2 else "act", Vm[:, j, :], PS[b][:], [PB(b)], [("Vm", j)])

            S.alias(["PT0", "PT1", "PT2", "recb", "on0", "sqd"], AR_W_NAMES)
            S.alias([("omT", a) for a in range(4)] + [("odT", a) for a in range(4)], ["RC", "RS", "ta", "tb"])
            nchunk = 4 * t + 4
            KT_ALL = [("KTn", tt_) for tt_ in range(t + 1)] + [("KTr", tt_) for tt_ in range(t + 1)]
            KD_ALL = [("KdT", tt_) for tt_ in range(t + 1)]
            ctr = {"sc": 0, "pt": 0}

            def qlo_of(j):
                return max(0, j - 4 * t) * 128

            items = []

            def mla_item(hh, j):
                accO, accS = (2, 3) if hh % 2 == 0 else (4, 5)
                po = (hh % 2) * 64
                pr = hh // 2
                ql = qlo_of(j)
                diag = j >= 4 * t
                st_ = {}

                def S_():
                    sbk = ctr["sc"] % 2
                    ctr["sc"] += 1
                    st_["sbk"] = sbk
                    mm(PS[sbk][:, ql:T], KT[0:96, hh, j * 128:(j + 1) * 128], QT[0:96, hh, ql:T],
                       True, not diag, KT_ALL + [("QT", hh)], [PB(sbk)])
                    if diag:
                        mm(PS[sbk][:, ql:ql + 128], identb[:], mskb[:], False, True, ["identb", "mskb"], [PB(sbk)])

                def E_():
                    sbk = st_["sbk"]
                    pk = ctr["pt"] % 3
                    ctr["pt"] += 1
                    st_["pk"] = pk
                    act(PT[pk][:, ql:T], PS[sbk][:, ql:T], AF.Exp, [PB(sbk)], [f"PT{pk}"], scale=MLA_SCALE)

                def V_():
                    pk = st_["pk"]
                    mm(PS[accO][:, ql:T], Vm[:, j, pr * 128:(pr + 1) * 128], PT[pk][:, ql:T], j == 0, j == nchunk - 1,
                       [("Vm", j), f"PT{pk}"], [PB(accO)])
                    mm(PS[accS][:, ql:T], onesb[:], PT[pk][:, ql:T], j == 0, j == nchunk - 1,
                       ["onesb", f"PT{pk}"], [PB(accS)])

                def post():
                    recip_act(recb[po:po + 64, :], PS[accS][po:po + 64, :], [PB(accS)], ["recb"])
                    tt("dve", omT[po:po + 64, pr, :], PS[accO][po:po + 64, :], recb[po:po + 64, :], ALU.mult,
                       [PB(accO), "recb"], [("omT", pr)])
                    return None

                return {"S": S_, "E": E_, "V": V_, "post": post if j == nchunk - 1 else None}

            def diff_item(hd, mp, j):
                m = 2 * hd + mp
                po = (m % 2) * 64
                ch = m // 2
                accO, accS = (2, 3) if mp == 0 else (4, 5)
                ql = qlo_of(j)
                jl = j - 4 * t
                near = jl >= -1
                st_ = {}

                def S_():
                    sbk = ctr["sc"] % 2
                    ctr["sc"] += 1
                    st_["sbk"] = sbk
                    mm(PS[sbk][:, ql:T], KdT[po:po + 64, ch, j * 128:(j + 1) * 128], QdT[po:po + 64, ch, ql:T],
                       True, not near, KD_ALL + [("QdT", ch)], [PB(sbk)])
                    if near:
                        blocks = []
                        if jl >= 0:
                            blocks.append((jl, 0))
                        if jl + 1 <= 3:
                            blocks.append((jl + 1, 128))
                        for bi, (qb, toff) in enumerate(blocks):
                            mm(PS[sbk][:, qb * 128:(qb + 1) * 128], identb[:], tzh[:, m, toff:toff + 128],
                               False, False, ["identb", "tzh"], [PB(sbk)])
                            mm(PS[sbk][:, qb * 128:(qb + 1) * 128], identb[:], tzl[:, m, toff:toff + 128],
                               False, bi == len(blocks) - 1, ["identb", "tzl"], [PB(sbk)])

                def E_():
                    sbk = st_["sbk"]
                    pk = ctr["pt"] % 3
                    ctr["pt"] += 1
                    st_["pk"] = pk
                    nearhi = min(T, (jl + 2) * 128) if near else ql
                    if nearhi > ql:
                        act(PT[pk][:, ql:nearhi], PS[sbk][:, ql:nearhi], AF.Exp, [PB(sbk)], [f"PT{pk}"])
                    if nearhi < T:
                        act(PT[pk][:, nearhi:T], PS[sbk][:, nearhi:T], AF.Exp, [PB(sbk), "colsb"], [f"PT{pk}"],
                            bias=B31(m))

                def V_():
                    pk = st_["pk"]
                    mm(PS[accO][:, ql:T], Vd[:, j, hd * 128:(hd + 1) * 128], PT[pk][:, ql:T], j == 0, j == nchunk - 1,
                       [("Vd", j), f"PT{pk}"], [PB(accO)])
                    mm(PS[accS][:, ql:T], onesb[:], PT[pk][:, ql:T], j == 0, j == nchunk - 1,
                       ["onesb", f"PT{pk}"], [PB(accS)])

                def post():
                    recip_act(recb, PS[accS][:], [PB(accS)], ["recb"])
                    if mp == 0:
                        tt("dve", on[0], PS[accO][:], recb, ALU.mult, [PB(accO), "recb"], ["on0"])
                        return None
                    tt("dve", recb, PS[accO][:], recb, ALU.mult, [PB(accO), "recb"], ["recb"])
                    stt("dve", on[0], recb, nlam, on[0], ALU.mult, ALU.add, ["recb", "nlam", "on0"], ["on0"])
                    tt("dve", sqd, on[0], on[0], ALU.mult, ["on0"], ["sqd"])

                    def deferred():
                        mm(PS[6][:], onesb[:], sqd, True, True, ["onesb", "sqd"], [PB(6)])
                        rsqrt_act(recb, PS[6][:], epsc[:, 3:4], [PB(6)], ["recb"])
                        stt("dve", odT[:, hd, :], on[0], gd2, recb, ALU.mult, ALU.mult, ["on0", "gd2", "recb"], [("odT", hd)])
                    return deferred

                return {"S": S_, "E": E_, "V": V_, "post": post if j == nchunk - 1 else None}

            for hh in range(8):
                for j in range(nchunk):
                    items.append(mla_item(hh, j))
            for hd in range(4):
                for mp in range(2):
                    for j in range(nchunk):
                        items.append(diff_item(hd, mp, j))
            run_pipeline(items)

            S.alias(["g0", "g1", "g2", "g3"], AR_W_NAMES)
            S.alias([("mixT", a) for a in range(8)], [("QT", a) for a in range(8)])
            for half in range(2):
                Ugm, rgm = load_unit(f"wgate_{half}")
                Ugd, rgd = load_unit(f"wgate_{2 + half}")
                Uup, rup = load_unit(f"wup_{half}")
                for dl in range(4):
                    d = half * 4 + dl
                    b0 = 4 * (d % 2)
                    bgm, bgd, bym, byd = b0, b0 + 1, b0 + 2, b0 + 3
                    for k in range(8):
                        mm(PS[bgm][:], Ugm[:, k, dl * 128:(dl + 1) * 128], hT[:, k, :], k == 0, k == 7, [rgm] + HT_ALL, [PB(bgm)])
                    for k in range(8):
                        mm(PS[bgd][:], Ugd[:, k, dl * 128:(dl + 1) * 128], hT[:, k, :], k == 0, k == 7, [rgd] + HT_ALL, [PB(bgd)])
                    for k in range(4):
                        mm(PS[bym][:], Uup[:, k, dl * 128:(dl + 1) * 128], omT[:, k, :], k == 0, k == 3, [rup, ("omT", k)], [PB(bym)])
                    for k in range(4):
                        mm(PS[byd][:], Uup[:, 4 + k, dl * 128:(dl + 1) * 128], odT[:, k, :], k == 0, k == 3, [rup, ("odT", k)], [PB(byd)])
                    act(gbuf[0], PS[bgm][:], AF.Sigmoid, [PB(bgm), "colsb"], ["g0"], bias=BGATE(d))
                    act(gbuf[1], PS[bgd][:], AF.Sigmoid, [PB(bgd), "colsb"], ["g1"], bias=BGATE(8 + d))
                    tt("dve", gbuf[2], gbuf[0], PS[bym][:], ALU.mult, ["g0", PB(bym)], ["g2"])
                    tt("dve", gbuf[3], gbuf[1], PS[byd][:], ALU.mult, ["g1", PB(byd)], ["g3"])
                    tt("pool", mixT[:, d, :], gbuf[2], gbuf[3], ALU.add, ["g2", "g3"], [("mixT", d)])
            down_proj("wo", mixT, "mixT", ln_cb)

        def mem_prep(seq):
            S.alias(["memf", "memb", "memT"], AR_A_NAMES)
            memf = arenaA[:, 0:4096].bitcast(F32).rearrange("p (a t) -> p a t", a=2)
            memb = arenaA[:, 4096:6144].rearrange("p (a t) -> p a t", a=2)
            memT = arenaA[:, 6144:8192].rearrange("p (a t) -> p a t", a=8)
            dma(memf, mem_d[seq * MEM:(seq + 1) * MEM, :].rearrange("(a p) d -> p a d", p=128), "memld", [], ["memf"])
            cp("pool", memb, memf, ["memf"], ["memb"])
            for a in range(2):
                bank = PS[a].bitcast(BF16)
                for k in range(8):
                    S.op("pe", lambda e, k=k, a=a, bank=bank: e.transpose(out=bank[:, k * 128:(k + 1) * 128],
                                                                        in_=memb[:, a, k * 128:(k + 1) * 128], identity=identb[:]),
                         reads=["memb", "identb"], writes=[PB(a)])
                cp("act", memT[:, :, a * 128:(a + 1) * 128], bank.rearrange("p (k t) -> p k t", k=8), [PB(a)], ["memT"])
            for g in range(2):
                U, r = load_unit(f"mkv_{g}")
                for cl in range(4):
                    c = g * 4 + cl
                    b = 2 + (c % 4)
                    for k in range(8):
                        mm(PS[b][:, 0:MEM], U[:, k, cl * 128:(cl + 1) * 128], memT[:, k, :], k == 0, k == 7, [r, "memT"], [PB(b)])
                    cp("act" if c % 2 else "dve", KmT[:, c, :], PS[b][:, 0:MEM], [PB(b)], ["KmT"])
            for g in range(2):
                U, r = load_unit(f"mkv_{2 + g}")
                for a in range(2):
                    b = 6 + a
                    for k in range(8):
                        mm(PS[b][:], memT[:, k, a * 128:(a + 1) * 128], U[:, k, :], k == 0, k == 7, [r, "memT"], [PB(b)])
                    cp("act" if a % 2 else "dve", Vmem[:, a, g * 512:(g + 1) * 512], PS[b][:], [PB(b)], ["Vmem"])

        def mem_attn(ln_cb):
            S.alias([("qmT", a) for a in range(8)] + [("omemT", a) for a in range(8)], AR_A_NAMES)
            S.alias(["PT0", "PT1", "PT2", "recb"], AR_W_NAMES)
            for g in range(2):
                U, r = load_unit(f"mq_{g}")
                for cl in range(4):
                    c = g * 4 + cl
                    b = c % 8
                    for k in range(8):
                        mm(PS[b][:], U[:, k, cl * 128:(cl + 1) * 128], hT[:, k, :], k == 0, k == 7, [r] + HT_ALL, [PB(b)])
                    cp("act" if c % 2 else "dve", qmT[:, c, :], PS[b][:], [PB(b)], [("qmT", c)])
            ctr = {"sc": 0, "pt": 0}

            def mem_item(hd, j):
                a0, a1, aS = (2, 3, 4) if hd % 2 == 0 else (5, 6, 7)
                st_ = {}

                def S_():
                    sbk = ctr["sc"] % 2
                    ctr["sc"] += 1
                    st_["sbk"] = sbk
                    for c in range(2):
                        mm(PS[sbk][:], KmT[:, 2 * hd + c, j * 128:(j + 1) * 128], qmT[:, 2 * hd + c, :], c == 0, c == 1,
                           ["KmT", ("qmT", 2 * hd + c)], [PB(sbk)])

                def E_():
                    pk = ctr["pt"] % 3
                    ctr["pt"] += 1
                    st_["pk"] = pk
                    act(PT[pk], PS[st_["sbk"]][:], AF.Exp, [PB(st_["sbk"])], [f"PT{pk}"], scale=MEM_SCALE)

                def V_():
                    pk = st_["pk"]
                    mm(PS[a0][:], Vmem[:, j, hd * 256:hd * 256 + 128], PT[pk], j == 0, j == 1, ["Vmem", f"PT{pk}"], [PB(a0)])
                    mm(PS[a1][:], Vmem[:, j, hd * 256 + 128:hd * 256 + 256], PT[pk], j == 0, j == 1, ["Vmem", f"PT{pk}"], [PB(a1)])
                    mm(PS[aS][:], onesb[:], PT[pk], j == 0, j == 1, ["onesb", f"PT{pk}"], [PB(aS)])

                def post():
                    recip_act(recb, PS[aS][:], [PB(aS)], ["recb"])
                    tt("dve", omemT[:, 2 * hd, :], PS[a0][:], recb, ALU.mult, [PB(a0), "recb"], [("omemT", 2 * hd)])
                    tt("dve", omemT[:, 2 * hd + 1, :], PS[a1][:], recb, ALU.mult, [PB(a1), "recb"], [("omemT", 2 * hd + 1)])
                    return None

                return {"S": S_, "E": E_, "V": V_, "post": post if j == 1 else None}

            run_pipeline([mem_item(hd, j) for hd in range(4) for j in range(2)])
            down_proj("mo", omemT, "omemT", ln_cb)

        out_tokens = []
        first = True
        ntl = len(tile_rows)
        load_xb(0)
        for sub_ in range(NSUB):
            next_T(sub_)
        hT_ready = {0}
        emit_casts_interleaved(("ffn1",))
        emit_casts(CAST_ORDER[3:9])
        tn = -1
        for seq in range(n_seq):
            for t in range(n_tiles):
                tn += 1
                row0 = seq * SEQ + t * T
                if tn not in hT_ready:
                    load_xb(tn)
                    for sub_ in range(NSUB):
                        next_T(sub_)
                    hT_ready.add(tn)
                dma(h[:], x_d[row0:row0 + T, :].rearrange("(s p) d -> p s d", p=128), "xld",
                    [], [("h", s) for s in range(NSUB)], eng="pool")
                load_lngb(0)
                if first:
                    cast_hook[0] = lambda: emit_casts(CAST_ORDER[9:12], after=[("gT", 0)])
                    cast_hook[1] = lambda: emit_casts_interleaved(("ffn2",), after=[("gT", 21)])
                vd_state = {}

                def vd_after(sub, t=t, vd_state=vd_state):
                    if "U" not in vd_state:
                        vd_state["U"], vd_state["r"] = load_unit("win_vd")
                    U_, r_ = vd_state["U"], vd_state["r"]
                    b = 2 * sub
                    for k in range(8):
                        mm(PS[b][:], hT[:, k, sub * 128:(sub + 1) * 128], U_[:, k, :], k == 0, k == 7,
                           [r_, ("hT", sub)], [PB(b)])
                    cp("dve" if sub % 2 else "act", Vd[:, 4 * t + sub, :], PS[b][:], [PB(b)], [("Vd", 4 * t + sub)])

                ffn("ffn1", LNPipe(C_FFN, stop_stage == 1, row0, 0, after_D=vd_after if stop_stage >= 2 else None))
                cast_hook[0] = cast_hook[1] = None
                first = False
                if stop_stage >= 2:
                    load_lngb(1)
                    mixer(t, t * T, LNPipe(C_MIX, stop_stage == 2, row0, 1))
                if stop_stage >= 3:
                    load_lngb(2)
                    if t == 0:
                        mem_prep(seq)
                    mem_attn(LNPipe(C_MIX, stop_stage == 3, row0, 2))
                if stop_stage >= 4:
                    load_lngb(3)
                    ln4 = LNPipe(C_FFN, True, row0, 3)
                    if tn + 1 < ntl:
                        next_hook[0] = lambda tn=tn: load_xb(tn + 1)
                        ln4.after_A = next_T
                        hT_ready.add(tn + 1)
                    ffn("ffn2", ln4)
        fin = []
        for s in range(NSUB):
            c = S.dma_sems.get(f"out{s}")
            if c:
                fin.append(("d", f"out{s}", c[0]))
        S.wait_all("pool", fin)
        S.emit()
    return nc


def _t5_bucket(n):
    n = np.maximum(n, 0)
    max_exact = 16
    nf = np.maximum(n, 1).astype(np.float32)
    large = max_exact + (np.log(nf / np.float32(max_exact)) / np.float32(math.log(128 / max_exact))
                         * np.float32(32 - max_exact)).astype(np.int32)
    large = np.minimum(large, 31)
    return np.where(n < max_exact, n, large)


def host_consts(inp):
    f32 = np.float32
    c = {}
    c["lnG"] = np.stack([np.broadcast_to(np.asarray(inp[f"ln{i}_g"], f32).reshape(1, D), (128, D)) for i in (1, 2, 3, 4)]).copy()
    c["lnB"] = np.stack([np.broadcast_to(np.asarray(inp[f"ln{i}_b"], f32).reshape(1, D), (128, D)) for i in (1, 2, 3, 4)]).copy()
    cols = np.zeros((128, 96), f32)
    cols[:, 0:3] = np.asarray(inp["q_norm_g"], f32).reshape(3, 128).T
    cols[:, 3:5] = np.asarray(inp["kv_norm_g"], f32).reshape(2, 128).T
    cols[:, 5:21] = np.asarray(inp["b_gate"], f32).reshape(16, 128).T
    cols[:, 21] = np.asarray(inp["diff_norm_g"], f32).reshape(128)
    rb = np.asarray(inp["rel_bias"], f32)
    cols[:, 22:30] = rb[31][None, :]
    for i_ in range(4):
        cols[:, 32 + i_ * 8:40 + i_ * 8] = np.asarray(inp[f"ln{i_ + 1}_g"], f32).reshape(8, 128).T
        cols[:, 64 + i_ * 8:72 + i_ * 8] = np.asarray(inp[f"ln{i_ + 1}_b"], f32).reshape(8, 128).T
    c["cols"] = cols
    c["lamv"] = np.stack([np.broadcast_to(np.asarray(inp[k], f32).reshape(1, 64), (128, 64))
                          for k in ("lam_q1", "lam_k1", "lam_q2", "lam_k2")], axis=1).copy()
    inv_freq = (np.float32(10000.0) ** (-np.arange(0, 32, 2, dtype=f32) / np.float32(32))).astype(f32)
    ang = np.arange(SEQ, dtype=f32)[:, None] * inv_freq[None, :]
    cosv, sinv = np.cos(ang).astype(f32), np.sin(ang).astype(f32)
    cs = np.zeros((128, 2, SEQ), f32)
    cs[64:80, 0] = cosv.T
    cs[80:96, 0] = cosv.T
    cs[64:80, 1] = -sinv.T
    cs[80:96, 1] = sinv.T
    c["cs"] = cs
    kk = np.arange(128)[:, None]
    cc = np.arange(256)[None, :]
    n = cc - kk
    bt = rb[_t5_bucket(n)]
    bt = np.where((n >= 0)[:, :, None], bt, f32(NEG))
    c["tz"] = np.ascontiguousarray(np.transpose(bt, (0, 2, 1))).astype(f32)
    q = np.arange(128)[None, :]
    c["msk"] = np.where(q >= kk, f32(0), f32(NEG)).astype(f32)
    c["idn"] = np.eye(128, dtype=f32)
    return c


def make_in_maps(inp, n_cores, n_seq):
    consts = host_consts(inp)
    wts = {n: np.ascontiguousarray(np.asarray(inp[n], np.float32).reshape(W_SHAPES[n])) for n in W_SHAPES}
    x = np.asarray(inp["x"], np.float32)
    mem = np.asarray(inp["mem"], np.float32)
    maps = []
    for cidx in range(n_cores):
        m = dict(wts)
        m.update(consts)
        m["x"] = np.ascontiguousarray(x[cidx * n_seq:(cidx + 1) * n_seq].reshape(n_seq * SEQ, D))
        m["mem"] = np.ascontiguousarray(mem[cidx * n_seq:(cidx + 1) * n_seq].reshape(n_seq * MEM, D))
        maps.append(m)
    return maps


_NC_CACHE = {}


def kernel(**inputs):
    n_cores, n_seq = 8, 2
    if "full" not in _NC_CACHE:
        _NC_CACHE["full"] = build(n_seq=n_seq, n_tiles=4, stop_stage=4)
    nc = _NC_CACHE["full"]
    maps = make_in_maps(inputs, n_cores, n_seq)
    res = run_bass_kernel_spmd(nc, maps, core_ids=list(range(n_cores)))
    outs = [np.asarray(r["out"]).reshape(n_seq, SEQ, D) for r in res.results]
    return np.concatenate(outs, axis=0).astype(np.float32)
```
